# Optimizing a Trainium2 kernel written in Bass

```python
import math
import jax, jax.numpy as jnp
from jax import lax
import numpy as np

D_MODEL = 1024
BATCH = 4
SEQ = 8192
DEPTH = 2

CTX_LEN = 256
GRID_W = 64
EPS = 1e-6
ROPE_THETA = 10000.0
NEG_INF = -1e30
Q_BLOCK = 128

MLA_HEADS = 8
MLA_NOPE = 64
MLA_ROPE = 32
MLA_V = 64
KV_RANK = 256
MLA_SCALE = (MLA_NOPE + MLA_ROPE) ** -0.5
SWA_HEADS = 8
SWA_KV_HEADS = 2
SWA_HEAD_DIM = 64
WINDOW = 128
BLOCK = 128
SWA_SCALE = SWA_HEAD_DIM ** -0.5
HYENA_WIDTH = 512
HYENA_ORDER = 2
HYENA_BANDS = 16
HYENA_EMB = 2 * HYENA_BANDS + 1
HYENA_HIDDEN = 64
HYENA_TARGET = 1e-2
HYENA_FAST_PCT = 0.3
HYENA_SLOW_PCT = 1.5
N_BRANCH = 3
BRANCH_WIDTH = 512
D_FF = 2816

COLS = (
    MLA_HEADS * (MLA_NOPE + MLA_ROPE),
    KV_RANK,
    MLA_ROPE,
    SWA_HEADS * SWA_HEAD_DIM,
    SWA_KV_HEADS * SWA_HEAD_DIM,
    SWA_KV_HEADS * SWA_HEAD_DIM,
    (HYENA_ORDER + 1) * HYENA_WIDTH,
    N_BRANCH * D_MODEL,
)
IN_WIDTH = sum(COLS)

kernel_name = "hybrid_mla_swa_hyena_dit_block"


def split_cols(p):
    outs, start = [], 0
    for w in COLS:
        outs.append(p[..., start:start + w])
        start += w
    return outs


def rms_norm(x, g):
    xf = x.astype(jnp.float32)
    y = xf * lax.rsqrt(jnp.mean(jnp.square(xf), axis=-1, keepdims=True) + EPS)
    return (y * g.astype(jnp.float32)).astype(x.dtype)


def modulate(x, g, shift, scale):
    return rms_norm(x, g) * (1 + scale) + shift


def axial_rope_tables(rows, rot_dim):
    row = jnp.repeat(jnp.arange(rows, dtype=jnp.float32), GRID_W)
    col = jnp.tile(jnp.arange(GRID_W, dtype=jnp.float32), rows)
    n_freq = rot_dim // 4
    inv_freq = ROPE_THETA ** (-jnp.arange(n_freq, dtype=jnp.float32) / n_freq)
    ang = jnp.concatenate([row[:, None] * inv_freq, col[:, None] * inv_freq], axis=-1)
    return jnp.cos(ang), jnp.sin(ang)


def apply_rope(x, cos, sin):
    x1, x2 = jnp.split(x, 2, axis=-1)
    return jnp.concatenate([x1 * cos - x2 * sin, x2 * cos + x1 * sin], axis=-1).astype(x.dtype)


def dwconv3(x, w, b):
    xp = jnp.pad(x, ((0, 0), (1, 1), (0, 0)))
    return xp[:, :-2] * w[0] + xp[:, 1:-1] * w[1] + xp[:, 2:] * w[2] + b


def sweep_query_blocks(fn, *qs):
    B, S = qs[0].shape[:2]
    nb = S // Q_BLOCK
    blocks = tuple(jnp.moveaxis(t.reshape(B, nb, Q_BLOCK, *t.shape[2:]), 1, 0) for t in qs)
    out = lax.map(lambda args: fn(*args), blocks)
    return jnp.moveaxis(out, 0, 1).reshape(B, S, -1)


def mla_kv(ckv, kv_norm_g, w_kv_up):
    B, L, _ = ckv.shape
    kv = (rms_norm(ckv, kv_norm_g) @ w_kv_up).reshape(B, L, MLA_HEADS, MLA_NOPE + MLA_V)
    return kv[..., :MLA_NOPE], kv[..., MLA_NOPE:]


def mla_attend(q_nope, q_rope, k_nope, k_rope, v):
    s = (jnp.einsum('bqhd,bkhd->bhqk', q_nope, k_nope, preferred_element_type=jnp.float32)
         + jnp.einsum('bqhr,bkr->bhqk', q_rope, k_rope, preferred_element_type=jnp.float32))
    p = jax.nn.softmax(s * MLA_SCALE, axis=-1).astype(v.dtype)
    return jnp.einsum('bhqk,bkhd->bqhd', p, v)


def swa_latent_attend(q, k, v, k_ctx, v_ctx, sink):
    B, S = q.shape[:2]
    C = k_ctx.shape[1]
    nb = S // BLOCK
    grp = SWA_HEADS // SWA_KV_HEADS
    qb = q.reshape(B, nb, BLOCK, SWA_KV_HEADS, grp, SWA_HEAD_DIM)

    def band(t):
        tp = jnp.pad(t, ((0, 0), (BLOCK, BLOCK), (0, 0), (0, 0))).reshape(B, nb + 2, BLOCK, SWA_KV_HEADS, SWA_HEAD_DIM)
        return jnp.concatenate([tp[:, :-2], tp[:, 1:-1], tp[:, 2:]], axis=2)

    kw, vw = band(k), band(v)
    s_loc = jnp.einsum('bnqhgd,bnkhd->bnhgqk', qb, kw, preferred_element_type=jnp.float32)
    s_ctx = jnp.einsum('bnqhgd,bchd->bnhgqc', qb, k_ctx, preferred_element_type=jnp.float32)
    q_pos = jnp.arange(nb)[:, None, None] * BLOCK + jnp.arange(BLOCK)[None, :, None]
    k_pos = (jnp.arange(nb)[:, None, None] - 1) * BLOCK + jnp.arange(3 * BLOCK)[None, None, :]
    valid = (jnp.abs(q_pos - k_pos) <= WINDOW) & (k_pos >= 0) & (k_pos < S)
    s_loc = jnp.where(valid[None, :, None, None], s_loc * SWA_SCALE, NEG_INF)
    sink_l = jnp.broadcast_to(sink.astype(jnp.float32).reshape(1, 1, SWA_KV_HEADS, grp, 1, 1), s_ctx.shape[:-1] + (1,))
    p = jax.nn.softmax(jnp.concatenate([s_ctx * SWA_SCALE, s_loc, sink_l], axis=-1), axis=-1).astype(v.dtype)
    out = (jnp.einsum('bnhgqc,bchd->bnqhgd', p[..., :C], v_ctx)
           + jnp.einsum('bnhgqk,bnkhd->bnqhgd', p[..., C:C + 3 * BLOCK], vw))
    return out.reshape(B, S, SWA_HEADS * SWA_HEAD_DIM)


def swa_context_attend(q, k, v, sink):
    B, C = q.shape[:2]
    grp = SWA_HEADS // SWA_KV_HEADS
    qg = q.reshape(B, C, SWA_KV_HEADS, grp, SWA_HEAD_DIM)
    s = jnp.einsum('bqhgd,bkhd->bhgqk', qg, k, preferred_element_type=jnp.float32) * SWA_SCALE
    sink_l = jnp.broadcast_to(sink.astype(jnp.float32).reshape(1, SWA_KV_HEADS, grp, 1, 1), s.shape[:-1] + (1,))
    p = jax.nn.softmax(jnp.concatenate([s, sink_l], axis=-1), axis=-1)[..., :-1].astype(v.dtype)
    return jnp.einsum('bhgqk,bkhd->bqhgd', p, v).reshape(B, C, SWA_HEADS * SWA_HEAD_DIM)


def hyena_filters(n_tokens, w1, b1, w2, b2, freq, w3):
    L = n_tokens
    t_idx = jnp.arange(L, dtype=jnp.float32)
    t = t_idx / max(L - 1, 1)
    bands = jnp.linspace(1e-4, HYENA_BANDS - 1, HYENA_BANDS, dtype=jnp.float32)
    ang = (2.0 * math.pi / L) * t_idx[:, None] * bands
    feats = jnp.concatenate([t[:, None], jnp.cos(ang), -jnp.sin(ang)], axis=-1)
    hid = jnp.sin(freq[0] * (feats @ w1 + b1))
    hid = jnp.sin(freq[1] * (hid @ w2 + b2))
    h = (hid @ w3).astype(jnp.float32).reshape(L, 2, HYENA_ORDER, HYENA_WIDTH)
    deltas = jnp.abs(jnp.linspace(math.log(HYENA_TARGET) / HYENA_SLOW_PCT, math.log(HYENA_TARGET) / HYENA_FAST_PCT,
                                  HYENA_WIDTH, dtype=jnp.float32))
    h = h * jnp.exp(-t[:, None] * deltas)[:, None, None, :]
    kern = jnp.concatenate([h[:, 0], jnp.zeros((1, HYENA_ORDER, HYENA_WIDTH), jnp.float32), h[:0:-1, 1]], axis=0)
    kern = kern / jnp.sum(jnp.abs(kern), axis=0, keepdims=True)
    return jnp.fft.rfft(kern, axis=0)


def hyena_branch(proj, conv_w, conv_b, filt_f, skip):
    L = proj.shape[1]
    u = dwconv3(proj, conv_w, conv_b)
    z, *gates = jnp.split(u, HYENA_ORDER + 1, axis=-1)
    z = z.astype(jnp.float32)
    for o in range(HYENA_ORDER):
        conv = jnp.fft.irfft(jnp.fft.rfft(z, n=2 * L, axis=1) * filt_f[:, o], n=2 * L, axis=1)[:, :L]
        z = gates[o] * (conv + z * skip[o])
    return z.astype(proj.dtype)


def merge_branches(y_a, y_b, y_c, gate_logits, w_branch, w_out):
    g_a, g_b, g_c = jnp.split(jax.nn.sigmoid(gate_logits), N_BRANCH, axis=-1)
    merged = g_a * (y_a @ w_branch[0]) + g_b * (y_b @ w_branch[1]) + g_c * (y_c @ w_branch[2])
    return merged @ w_out


def token_mixer(h, hc, with_ctx, rope_mla, rope_swa, w_in, kv_norm_g, w_kv_up, swa_sink,
                hy_conv_w, hy_conv_b, hy_w1, hy_b1, hy_w2, hy_b2, hy_freq, hy_w3, hy_skip, w_branch, w_out):
    B, S, _ = h.shape
    C = hc.shape[1]
    mq, mckv, mkr, sq, sk, sv, hy, gt = split_cols(h @ w_in)
    mq_c, mckv_c, mkr_c, sq_c, sk_c, sv_c, hy_c, gt_c = split_cols(hc @ w_in)
    cos_m, sin_m = rope_mla
    cos_s, sin_s = rope_swa

    q = mq.reshape(B, S, MLA_HEADS, MLA_NOPE + MLA_ROPE)
    q_nope = q[..., :MLA_NOPE]
    q_rope = apply_rope(q[..., MLA_NOPE:], cos_m[:, None], sin_m[:, None])
    k_nope, v = mla_kv(mckv, kv_norm_g, w_kv_up)
    kc_nope, vc = mla_kv(mckv_c, kv_norm_g, w_kv_up)
    keys_nope = jnp.concatenate([kc_nope, k_nope], axis=1)
    keys_rope = jnp.concatenate([mkr_c, apply_rope(mkr, cos_m, sin_m)], axis=1)
    vals = jnp.concatenate([vc, v], axis=1)
    y_a = sweep_query_blocks(lambda qn, qr: mla_attend(qn, qr, keys_nope, keys_rope, vals), q_nope, q_rope)

    q_s = apply_rope(sq.reshape(B, S, SWA_HEADS, SWA_HEAD_DIM), cos_s[:, None], sin_s[:, None])
    k_s = apply_rope(sk.reshape(B, S, SWA_KV_HEADS, SWA_HEAD_DIM), cos_s[:, None], sin_s[:, None])
    v_s = sv.reshape(B, S, SWA_KV_HEADS, SWA_HEAD_DIM)
    k_sc = sk_c.reshape(B, C, SWA_KV_HEADS, SWA_HEAD_DIM)
    v_sc = sv_c.reshape(B, C, SWA_KV_HEADS, SWA_HEAD_DIM)
    y_b = swa_latent_attend(q_s, k_s, v_s, k_sc, v_sc, swa_sink)

    y_c = hyena_branch(hy, hy_conv_w, hy_conv_b, hyena_filters(S, hy_w1, hy_b1, hy_w2, hy_b2, hy_freq, hy_w3), hy_skip)

    y = merge_branches(y_a, y_b, y_c, gt, w_branch, w_out)
    if not with_ctx:
        return y, None

    q_c = mq_c.reshape(B, C, MLA_HEADS, MLA_NOPE + MLA_ROPE)
    yc_a = mla_attend(q_c[..., :MLA_NOPE], q_c[..., MLA_NOPE:], kc_nope, mkr_c, vc).reshape(B, C, -1)
    yc_b = swa_context_attend(sq_c.reshape(B, C, SWA_HEADS, SWA_HEAD_DIM), k_sc, v_sc, swa_sink)
    yc_c = hyena_branch(hy_c, hy_conv_w, hy_conv_b, hyena_filters(C, hy_w1, hy_b1, hy_w2, hy_b2, hy_freq, hy_w3), hy_skip)
    y_ctx = merge_branches(yc_a, yc_b, yc_c, gt_c, w_branch, w_out)
    return y, y_ctx


def conv_ffn(h, w_up, conv_w, conv_b, w_down):
    u = dwconv3(h @ w_up, conv_w, conv_b)
    a, b = jnp.split(u, 2, axis=-1)
    return (jax.nn.silu(a) * b) @ w_down


def setup_inputs(seed: int = 0) -> dict:
    key = jax.random.key(seed)
    ks = iter(jax.random.split(key, 32))

    def nrm(shape, scale):
        return jax.random.normal(next(ks), shape, jnp.float32) * scale

    D = D_MODEL
    return {
        "x": nrm((BATCH, SEQ, D), 1.0),
        "c": nrm((BATCH, D), 1.0),
        "ctx": nrm((BATCH, CTX_LEN, D), 1.0),
        "c_ctx": nrm((D,), 1.0),
        "w_mod": nrm((DEPTH, D, 6 * D), D ** -0.5),
        "b_mod": nrm((DEPTH, 6 * D), 0.02),
        "norm_g": 1.0 + nrm((DEPTH, 4, D), 0.05),
        "w_in": nrm((DEPTH, D, IN_WIDTH), D ** -0.5),
        "kv_norm_g": 1.0 + nrm((DEPTH, KV_RANK), 0.05),
        "w_kv_up": nrm((DEPTH, KV_RANK, MLA_HEADS * (MLA_NOPE + MLA_V)), KV_RANK ** -0.5),
        "swa_sink": nrm((DEPTH, SWA_HEADS), 0.5),
        "hy_conv_w": nrm((DEPTH, 3, (HYENA_ORDER + 1) * HYENA_WIDTH), 3 ** -0.5),
        "hy_conv_b": nrm((DEPTH, (HYENA_ORDER + 1) * HYENA_WIDTH), 0.02),
        "hy_w1": nrm((DEPTH, HYENA_EMB, HYENA_HIDDEN), HYENA_EMB ** -0.5),
        "hy_b1": nrm((DEPTH, HYENA_HIDDEN), 0.02),
        "hy_w2": nrm((DEPTH, HYENA_HIDDEN, HYENA_HIDDEN), HYENA_HIDDEN ** -0.5),
        "hy_b2": nrm((DEPTH, HYENA_HIDDEN), 0.02),
        "hy_freq": 1.0 + nrm((DEPTH, 2, HYENA_HIDDEN), 0.05),
        "hy_w3": nrm((DEPTH, HYENA_HIDDEN, 2 * HYENA_ORDER * HYENA_WIDTH), HYENA_HIDDEN ** -0.5),
        "hy_skip": nrm((DEPTH, HYENA_ORDER, HYENA_WIDTH), 0.5),
        "w_branch": nrm((DEPTH, N_BRANCH, BRANCH_WIDTH, D), BRANCH_WIDTH ** -0.5),
        "w_out": nrm((DEPTH, D, D), D ** -0.5),
        "w_up": nrm((DEPTH, D, 2 * D_FF), D ** -0.5),
        "ffn_conv_w": nrm((DEPTH, 3, 2 * D_FF), 3 ** -0.5),
        "ffn_conv_b": nrm((DEPTH, 2 * D_FF), 0.02),
        "w_down": nrm((DEPTH, D_FF, D), D_FF ** -0.5),
    }


def reference(x, c, ctx, c_ctx, w_mod, b_mod, norm_g, w_in, kv_norm_g, w_kv_up, swa_sink,
              hy_conv_w, hy_conv_b, hy_w1, hy_b1, hy_w2, hy_b2, hy_freq, hy_w3, hy_skip,
              w_branch, w_out, w_up, ffn_conv_w, ffn_conv_b, w_down):
    S = x.shape[1]
    rows = S // GRID_W
    rope_mla = axial_rope_tables(rows, MLA_ROPE)
    rope_swa = axial_rope_tables(rows, SWA_HEAD_DIM)
    x_lat, x_ctx = x, ctx
    for l in range(DEPTH):
        with_ctx = l < DEPTH - 1
        mod = (jax.nn.silu(c) @ w_mod[l] + b_mod[l])[:, None, :]
        mod_c = (jax.nn.silu(c_ctx) @ w_mod[l] + b_mod[l])[None, None, :]
        sh1, sc1, g1, sh2, sc2, g2 = jnp.split(mod, 6, axis=-1)
        csh1, csc1, cg1, csh2, csc2, cg2 = jnp.split(mod_c, 6, axis=-1)

        y, y_ctx = token_mixer(
            modulate(x_lat, norm_g[l, 0], sh1, sc1), modulate(x_ctx, norm_g[l, 0], csh1, csc1), with_ctx,
            rope_mla, rope_swa, w_in[l], kv_norm_g[l], w_kv_up[l], swa_sink[l],
            hy_conv_w[l], hy_conv_b[l], hy_w1[l], hy_b1[l], hy_w2[l], hy_b2[l], hy_freq[l], hy_w3[l], hy_skip[l],
            w_branch[l], w_out[l])
        x_lat = x_lat + g1 * rms_norm(y, norm_g[l, 1])

        f = conv_ffn(modulate(x_lat, norm_g[l, 2], sh2, sc2), w_up[l], ffn_conv_w[l], ffn_conv_b[l], w_down[l])
        x_lat = x_lat + g2 * rms_norm(f, norm_g[l, 3])

        if with_ctx:
            x_ctx = x_ctx + cg1 * rms_norm(y_ctx, norm_g[l, 1])
            fc = conv_ffn(modulate(x_ctx, norm_g[l, 2], csh2, csc2), w_up[l], ffn_conv_w[l], ffn_conv_b[l], w_down[l])
            x_ctx = x_ctx + cg2 * rms_norm(fc, norm_g[l, 3])
    return x_lat
```

```python
import math
import os
import numpy as np
import ml_dtypes
import concourse.bass as bass
import concourse.mybir as mybir
from concourse.bass_utils import run_bass_kernel_spmd
from contextlib import ExitStack

F32 = mybir.dt.float32
BF16 = mybir.dt.bfloat16
AF = mybir.ActivationFunctionType
ALU = mybir.AluOpType
AX = mybir.AxisListType

D = 1024
CTX = 256
GRID_W = 64
EPS = 1e-6
THETA = 10000.0
MLA_H, MLA_NOPE, MLA_ROPE, MLA_V, KV_RANK = 8, 64, 32, 64, 256
MLA_SCALE = (MLA_NOPE + MLA_ROPE) ** -0.5
SWA_H, SWA_KV, SWA_D = 8, 2, 64
SWA_SCALE = SWA_D ** -0.5
HY_W, HY_BANDS, HY_HID = 512, 16, 64
HY_EMB = 2 * HY_BANDS + 1
D_FF = 2816
IN_W = 6432
C_MQ, C_CKV, C_KR, C_SQ, C_SK, C_SV, C_HY, C_GT = 0, 768, 1024, 1056, 1568, 1696, 1824, 3360
DEPTH = 2

TRACKS = ["pe", "dve", "act", "pool", "dq0", "dq1"]
QOF = {"pe": "tensor", "dve": "vector", "act": "scalar", "pool": "gpsimd", "dq0": "sync", "dq1": "sync"}
QUEUES = ["tensor", "vector", "scalar", "gpsimd", "sync"]
ISDMA = {"dq0": True, "dq1": True}
SAME_SYNC = {"pe": False, "dve": True, "act": True, "pool": True, "dq0": True, "dq1": True}


class Buf:
    __slots__ = ("name", "w", "r")

    def __init__(self, name=""):
        self.name = name
        self.w = None
        self.r = {}


NS = 8


class Sched:
    def __init__(self, nc, es):
        self.nc = nc
        self.sems = {}
        for t in TRACKS:
            if ISDMA.get(t, False):
                self.sems[t] = [es.enter_context(nc.semaphore("s_%s%d" % (t, i))) for i in range(NS)]
            else:
                self.sems[t] = es.enter_context(nc.semaphore("s_" + t))
        self.cnt = {t: 0 for t in TRACKS}
        self.seen = {q: {} for q in QUEUES}
        self.ops = {q: [] for q in QUEUES}
        self.nops = 0

    def _key(self, t, c):
        if ISDMA.get(t, False):
            slot = (c - 1) % NS
            use = (c - 1) // NS + 1
            return (t, slot), use, self.sems[t][slot], 16 * use
        return t, c, self.sems[t], c

    def _want(self, q, t, c, waits):
        key, lvl, sem, val = self._key(t, c)
        if self.seen[q].get(key, 0) < lvl:
            self.seen[q][key] = lvl
            waits.append((sem, val))

    def op(self, track, fn, r=(), w=()):
        q = QOF[track]
        need = {}
        dma = ISDMA.get(track, False)

        def req(dep):
            if dep is None:
                return
            t, c = dep
            if t == track and not SAME_SYNC[track]:
                return
            if ISDMA.get(t, False):
                need[(t, c)] = c
            elif c > need.get(t, 0):
                need[t] = c
        for b in r:
            req(b.w)
        for b in w:
            req(b.w)
            for (t, c) in b.r.values():
                if t != track or dma:
                    req((t, c))
        waits = []
        for k, c in need.items():
            t = k[0] if isinstance(k, tuple) else k
            self._want(q, t, c, waits)
        self.cnt[track] += 1
        c = self.cnt[track]
        if dma and c > NS:
            self._want(q, track, c - NS, waits)
        for b in r:
            if dma:
                b.r[(track, c)] = (track, c)
            else:
                b.r[track] = (track, c)
        for b in w:
            b.w = (track, c)
            b.r = {}
        if dma:
            _, _, sem, _ = self._key(track, c)
            inc = 16
        else:
            sem, inc = self.sems[track], 1
        self.ops[q].append((waits, fn, sem, inc))
        self.nops += 1

    def barrier(self):
        for q in QUEUES:
            waits = []
            for t in TRACKS:
                n = self.cnt[t]
                if n == 0:
                    continue
                if ISDMA.get(t, False):
                    for c in range(max(1, n - NS + 1), n + 1):
                        self._want(q, t, c, waits)
                else:
                    self._want(q, t, n, waits)
            if waits:
                self.ops[q].append((waits, None, None, None))

    def emit(self):
        with self.nc.Block() as block:
            for q in QUEUES:
                ops = self.ops[q]

                def body(engine, ops=ops):
                    for waits, fn, sem, inc in ops:
                        for (s_, v) in waits:
                            engine.wait_ge(s_, v)
                        if fn is not None:
                            fn(engine).then_inc(sem, inc)
                getattr(block, q)(body)
        self.ops = {q: [] for q in QUEUES}

    def flush(self):
        self.barrier()
        self.emit()


class TL:
    __slots__ = ("t", "b")

    def __init__(self, t, name=""):
        self.t = t
        self.b = Buf(name)

    def __getitem__(self, k):
        return self.t[k]


def bcast_mid(ap, n):
    sh = list(ap.shape)
    return ap.unsqueeze(1).to_broadcast([sh[0], n] + sh[1:])


def _bf(a):
    return np.ascontiguousarray(np.asarray(a, np.float32).astype(ml_dtypes.bfloat16))


def _f32(a):
    return np.ascontiguousarray(np.asarray(a, np.float32))


def rope_tables(pos, rot_dim):
    row = (pos // GRID_W).astype(np.float32)
    col = (pos % GRID_W).astype(np.float32)
    n_freq = rot_dim // 4
    inv = (np.float32(THETA) ** (-np.arange(n_freq, dtype=np.float32) / np.float32(n_freq))).astype(np.float32)
    ang = np.concatenate([row[:, None] * inv, col[:, None] * inv], axis=-1).astype(np.float32)
    return np.cos(ang).astype(np.float32), np.sin(ang).astype(np.float32)


def tile_major(tab):
    T, F = tab.shape
    return _f32(tab.reshape(T // 128, 128, F).transpose(1, 0, 2))


def fft_tables(L):
    H1 = L // 128
    N1 = 2 * H1
    N = 128 * N1
    KH = H1 + 1
    n1 = np.arange(H1)[:, None].astype(np.float64)
    k1 = np.arange(KH)[None, :].astype(np.float64)
    a = 2 * np.pi * n1 * k1 / N1
    F1 = np.concatenate([np.cos(a), -np.sin(a)], axis=1)
    n2 = np.arange(128).astype(np.float64)
    k2 = np.arange(128).astype(np.float64)
    kk = np.arange(KH).astype(np.float64)
    th = 2 * np.pi * n2[:, None, None] * (kk[None, :, None] + N1 * k2[None, None, :]) / N
    G = np.stack([np.cos(th), -np.sin(th)], axis=2)
    ph = 2 * np.pi * k2[:, None] * n2[None, :] / 128.0
    Finv = np.stack([-np.sin(ph), np.cos(ph), np.sin(ph)], axis=1)
    alpha = np.full(KH, 2.0)
    alpha[0] = 1.0
    alpha[H1] = 1.0
    nn1 = np.arange(H1).astype(np.float64)
    tp = 2 * np.pi * (128 * nn1[None, None, :] + n2[None, :, None]) * kk[:, None, None] / N
    P = np.stack([alpha[:, None, None] / N * np.cos(tp), -alpha[:, None, None] / N * np.sin(tp)], axis=2)
    return dict(F1=_bf(F1), G=_bf(G), Finv=_bf(Finv), P=_bf(P))


def hyena_consts(L):
    t_idx = np.arange(L, dtype=np.float32)
    t = t_idx / np.float32(max(L - 1, 1))
    bands = np.linspace(1e-4, HY_BANDS - 1, HY_BANDS, dtype=np.float32)
    ang = (np.float32(2.0 * math.pi / L) * t_idx[:, None] * bands).astype(np.float32)
    feats = np.concatenate([t[:, None], np.cos(ang), -np.sin(ang)], axis=-1).astype(np.float32)
    deltas = np.abs(np.linspace(math.log(1e-2) / 1.5, math.log(1e-2) / 0.3, HY_W, dtype=np.float32))
    dsc = (-deltas / np.float32(max(L - 1, 1))).astype(np.float32)
    dsc_t = dsc.reshape(4, 128).T
    CH = min(512, L)
    nch = L // CH
    dbias = dsc_t[:, :, None] * (np.arange(nch, dtype=np.float32) * CH)[None, None, :]
    iota = np.broadcast_to(np.arange(CH, dtype=np.float32)[None, :], (128, CH))
    return dict(featsT=_f32(feats.T), dsc=_f32(dsc_t), dbias=_f32(dbias), iota=_f32(iota))


class Seq:
    pass


class Prog:
    def __init__(self, L, NH=1, depth=DEPTH, dbg=(), stop_after=None):
        self.L = L
        self.NH = NH
        self.NQ = L // NH
        self.depth = depth
        self.dbg = set(dbg)
        self.stop_after = stop_after
        self.nc = bass.Bass("TRN2", target_bir_lowering=False)
        self.consts = {}
        self._hyc = {}
        self.ins = {}
        self.outs = {}
        self.NK = CTX + L

    def inp(self, name, shape, dt=F32):
        ap = self.nc.dram_tensor(name, list(shape), dt, kind="ExternalInput").ap()
        self.ins[name] = ap
        return ap

    def const(self, name, arr):
        dt = BF16 if arr.dtype == ml_dtypes.bfloat16 else F32
        self.consts[name] = arr
        return self.inp(name, arr.shape, dt)

    def scr(self, name, shape, dt=F32):
        kind = "ExternalOutput" if name in self.dbg else "Internal"
        ap = self.nc.dram_tensor(name, list(shape), dt, kind=kind).ap()
        if name in self.dbg:
            self.outs[name] = ap
        return ap

    def T(self, ph, name, shape, dt, psum=False):
        self._tn = getattr(self, "_tn", 0) + 1
        nm = f"{name}_{self._tn}"
        if psum:
            t = ph.enter_context(self.nc.psum_tensor(nm, list(shape), dt))
        else:
            t = ph.enter_context(self.nc.sbuf_tensor(nm, list(shape), dt))
        return TL(t, nm)

    def dma(self, track, out_ap, in_ap, r=(), w=()):
        self.S.op(track, lambda e: e.dma_start(out=out_ap, in_=in_ap), r=r, w=w)

    def tt(self, track, out_ap, in0, in1, op, r=(), w=()):
        self.S.op(track, lambda e: e.tensor_tensor(out=out_ap, in0=in0, in1=in1, op=op), r=r, w=w)

    def ts(self, track, out_ap, in0, s1, s2, op0, op1=None, r=(), w=(), accum=None):
        if op1 is None:
            self.S.op(track, lambda e: e.tensor_scalar(out=out_ap, in0=in0, scalar1=s1, scalar2=None, op0=op0),
                      r=r, w=w)
        else:
            self.S.op(track, lambda e: e.tensor_scalar(out=out_ap, in0=in0, scalar1=s1, scalar2=s2, op0=op0,
                                                         op1=op1), r=r, w=w)

    def stt(self, track, out_ap, in0, scalar, in1, op0, op1, r=(), w=()):
        self.S.op(track, lambda e: e.scalar_tensor_tensor(out=out_ap, in0=in0, scalar=scalar, in1=in1,
                                                           op0=op0, op1=op1), r=r, w=w)

    def cp(self, track, out_ap, in_ap, r=(), w=()):
        if track == "act":
            self.S.op(track, lambda e: e.copy(out=out_ap, in_=in_ap), r=r, w=w)
        else:
            self.S.op(track, lambda e: e.tensor_copy(out=out_ap, in_=in_ap), r=r, w=w)

    def act(self, out_ap, in_ap, func, r=(), w=(), bias=None, scale=None, accum=None):
        kw = {}
        if bias is not None:
            kw["bias"] = bias
        if scale is not None:
            kw["scale"] = scale
        if accum is not None:
            kw["accum_out"] = accum
        self.S.op("act", lambda e: e.activation(out=out_ap, in_=in_ap, func=func, **kw), r=r, w=w)

    def mm(self, out_ap, lhsT, rhs, start, stop, r=(), w=()):
        self.S.op("pe", lambda e: e.matmul(out_ap, lhsT=lhsT, rhs=rhs, start=start, stop=stop), r=r, w=w)

    def tr(self, out_ap, in_ap, ident, r=(), w=()):
        self.S.op("pe", lambda e: e.transpose(out_ap, in_ap, ident), r=r, w=w)

    def memset(self, track, ap, val, w=()):
        self.S.op(track, lambda e: e.memset(ap, val), w=w)

    def rstd(self, ph_tiles, ss, n, out):
        t1, t2 = ph_tiles
        self.ts("dve", t1[:], ss[:], 1.0 / n, EPS, ALU.mult, ALU.add, r=[ss.b], w=[t1.b])
        self.act(t2[:], t1[:], AF.Sqrt, r=[t1.b], w=[t2.b])
        self.S.op("dve", lambda e: e.reciprocal(out=out[:], in_=t2[:]), r=[t2.b], w=[out.b])

    def load_w(self, ph, dst, src, KC, ncols, stages, cw=None, dstb=None):
        if cw is None:
            cw = max(64, min(512, (4096 // KC) // 64 * 64))
        i = 0
        for c0 in range(0, ncols, cw):
            c1 = min(ncols, c0 + cw)
            st = stages[i % len(stages)]
            self.dma("dq0", st[:, 0:KC, 0:c1 - c0], src[:, c0:c1].rearrange("(k p) c -> p k c", p=128),
                     w=[st.b])
            trk = "dve" if i % 2 == 0 else "pool"
            self.cp(trk, dst[:, :, c0:c1], st[:, 0:KC, 0:c1 - c0], r=[st.b], w=[dstb if dstb is not None else dst.b])
            i += 1

    def phase_mod(self, l):
        with ExitStack() as ph:
            T = lambda *a, **k: self.T(ph, *a, **k)
            sil = T("sil", [128, 8, 2], F32)
            brow = T("brow", [2, 6 * D], F32)
            ng = T("ng", [2, 4, D], F32)
            mod = T("mod", [2, 6 * D], F32)
            tab = T("tab", [2, 6, D], F32)
            wst = [T("wst", [128, 8, 512], F32) for _ in range(3)]
            ps = [T("psm", [128, 512], F32, psum=True) for _ in range(2)]
            self.dma("dq0", sil[:], self.cT[:, :, :], w=[sil.b])
            self.dma("dq0", brow[:], self.b_mod[l].partition_broadcast(2), w=[brow.b])
            self.dma("dq0", ng[:], self.norm_g[l].rearrange("a d -> (a d)").partition_broadcast(2)
                     .rearrange("p (a d) -> p a d", a=4), w=[ng.b])
            self.act(sil[:], sil[:], AF.Silu, r=[sil.b], w=[sil.b])
            for j in range(12):
                st = wst[j % 3]
                self.dma("dq0", st[:], self.w_mod[l][:, j * 512:(j + 1) * 512]
                         .rearrange("(k p) c -> p k c", p=128), w=[st.b])
                p = ps[j % 2]
                for k in range(8):
                    self.mm(p[0:2, :], sil[:, k, :], st[:, k, :], k == 0, k == 7, r=[sil.b, st.b], w=[p.b])
                self.tt("dve", mod[:, j * 512:(j + 1) * 512], p[0:2, :], brow[:, j * 512:(j + 1) * 512],
                        ALU.add, r=[p.b, brow.b], w=[mod.b])
            m = lambda i: mod[:, i * D:(i + 1) * D]
            self.stt("dve", tab[:, 0, :], m(1), 1.0, ng[:, 0, :], ALU.add, ALU.mult, r=[mod.b, ng.b], w=[tab.b])
            self.cp("dve", tab[:, 1, :], m(0), r=[mod.b], w=[tab.b])
            self.tt("dve", tab[:, 2, :], m(2), ng[:, 1, :], ALU.mult, r=[mod.b, ng.b], w=[tab.b])
            self.stt("dve", tab[:, 3, :], m(4), 1.0, ng[:, 2, :], ALU.add, ALU.mult, r=[mod.b, ng.b], w=[tab.b])
            self.cp("dve", tab[:, 4, :], m(3), r=[mod.b], w=[tab.b])
            self.tt("dve", tab[:, 5, :], m(5), ng[:, 3, :], ALU.mult, r=[mod.b, ng.b], w=[tab.b])
            self.dma("dq0", self.MODT[l], tab[:], r=[tab.b], w=[self.b_modt])
            self.S.flush()

    def modrow(self, l, s, i):
        return self.MODT[l][s, i, :].partition_broadcast(128)

    def phase_a1(self, l, sq):
        L = sq.L
        nt = L // 128
        with ExitStack() as ph:
            T = lambda *a, **k: self.T(ph, *a, **k)
            dbl = lambda *a, **k: [self.T(ph, *a, **k) for _ in range(2)]
            Wc = T("Wc", [128, 8, 1824], BF16)
            Wkvg = T("Wkvg", [128, 2, 1024], BF16)
            kvg = T("kvg", [128, 2], F32)
            ident = T("ident", [128, 128], BF16)
            a1row = T("a1row", [128, D], F32)
            sh1row = T("sh1row", [128, D], F32)
            self.dma("dq0", ident[:], self.c_ident[:, :], w=[ident.b])
            self.dma("dq0", a1row[:], self.modrow(l, sq.set, 0), r=[self.b_modt], w=[a1row.b])
            self.dma("dq0", sh1row[:], self.modrow(l, sq.set, 1), r=[self.b_modt], w=[sh1row.b])
            self.dma("dq0", kvg[:], self.kv_norm_g[l], w=[kvg.b])
            with ExitStack() as ph2:
                stages = [self.T(ph2, "wstage", [128, 8, 512], F32) for _ in range(2)]
                self.load_w(ph2, Wc, self.w_in[l][:, 0:1824], 8, 1824, stages)
                st = stages[0]
                self.dma("dq0", st[:, 0:2, 0:512], self.w_kv_up[l][:, 0:512].rearrange("(k p) c -> p k c", p=128),
                         w=[st.b])
                st1 = stages[1]
                self.dma("dq0", st1[:, 0:2, 0:512], self.w_kv_up[l][:, 512:1024]
                         .rearrange("(k p) c -> p k c", p=128), w=[st1.b])
                for k in range(2):
                    self.ts("dve", Wkvg[:, k, 0:512], st[:, k, 0:512], kvg[:, k:k + 1], None, ALU.mult,
                            r=[st.b, kvg.b], w=[Wkvg.b])
                    self.ts("dve", Wkvg[:, k, 512:1024], st1[:, k, 0:512], kvg[:, k:k + 1], None, ALU.mult,
                            r=[st1.b, kvg.b], w=[Wkvg.b])
                self.S.flush()
            if sq.rope:
                cosM = T("cosM", [128, nt, 16], F32)
                sinM = T("sinM", [128, nt, 16], F32)
                cosS = T("cosS", [128, nt, 32], F32)
                sinS = T("sinS", [128, nt, 32], F32)
                for tl, src in ((cosM, self.c_cosM), (sinM, self.c_sinM), (cosS, self.c_cosS), (sinS, self.c_sinS)):
                    self.dma("dq0", tl[:], src[:, :, :], w=[tl.b])
            xt = dbl("xt", [128, D], F32)
            junk = T("junk", [128, D], BF16)
            tmp = dbl("tmp", [128, D], F32)
            hb = dbl("hb", [128, D], BF16)
            hT = dbl("hT", [128, 8, 128], BF16)
            st_ = lambda n: dbl(n, [128, 1], F32)
            ss, s1, s2, rs = st_("ss"), st_("s1"), st_("s2"), st_("rs")
            ssk, k1, k2, rk = st_("ssk"), st_("k1"), st_("k2"), st_("rk")
            ckvn = dbl("ckvn", [128, 256], BF16)
            ckvnT = dbl("ckvnT", [128, 2, 128], BF16)
            Kaug = dbl("Kaug", [128, 8, 96], BF16)
            Vt = dbl("Vt", [128, 8, 65], BF16)
            krr = dbl("krr", [128, 32], F32)
            rt = [dbl("rt%d" % i, [128, 256], F32) for i in range(4)]
            KTs = dbl("KTs", [128, 8, 128], BF16)
            skr = dbl("skr", [128, 128], BF16)
            KsT = dbl("KsT", [128, 128], BF16)
            svt = dbl("svt", [128, 128], BF16)
            Qaug = dbl("Qaug", [128, 8, 96], BF16)
            QTs = dbl("QTs", [128, 8, 128], BF16)
            sqr = dbl("sqr", [128, 512], BF16)
            QsT = dbl("QsT", [128, 4, 128], BF16)
            pT0 = T("pT0", [128, 1024], BF16, psum=True)
            pT1 = T("pT1", [128, 1024], BF16, psum=True)
            pT2 = T("pT2", [128, 1024], BF16, psum=True)
            pF = [T("pF%d" % i, [128, 512], F32, psum=True) for i in range(5)]
            pT2a = pT2b = pT2c = pT2.b

            def rope(xps, psb, nh, half, cos_tl, sin_tl, out_ap, j):
                t1, t2, t3, t4 = (rt[i][j % 2] for i in range(4))
                n = nh * half
                v = lambda t: t[:, 0:n].rearrange("p (h d) -> p h d", d=half)
                x1 = xps[:, :, 0:half]
                x2 = xps[:, :, half:2 * half]
                cb = bcast_mid(cos_tl[:, j, :], nh)
                sb = bcast_mid(sin_tl[:, j, :], nh)
                rb = [psb, cos_tl.b, sin_tl.b]
                self.tt("dve", v(t1), x1, cb, ALU.mult, r=rb, w=[t1.b])
                self.tt("dve", v(t2), x2, sb, ALU.mult, r=rb, w=[t2.b])
                self.tt("dve", v(t3), x2, cb, ALU.mult, r=rb, w=[t3.b])
                self.tt("dve", v(t4), x1, sb, ALU.mult, r=rb, w=[t4.b])
                return (t1, t2, t3, t4, v)

            for b_ in range(2):
                self.memset("dve", Vt[b_][:, :, 64:65], 1.0, w=[Vt[b_].b])
            CUT = float(os.environ.get("A1CUT", "9"))
            for j in range(nt):
                b = j % 2
                tok = slice(j * 128, (j + 1) * 128)
                X, TMP, HB, HT = xt[b], tmp[b], hb[b], hT[b]
                self.dma("dq0", X[:], sq.x[tok, :], r=[sq.xb], w=[X.b])
                self.act(junk[:], X[:], AF.Square, r=[X.b], w=[junk.b, ss[b].b], accum=ss[b][:])
                self.rstd((s1[b], s2[b]), ss[b], D, rs[b])
                self.stt("dve", TMP[:], X[:], rs[b][:], a1row[:], ALU.mult, ALU.mult,
                         r=[X.b, rs[b].b, a1row.b], w=[TMP.b])
                self.tt("pool", HB[:], TMP[:], sh1row[:], ALU.add, r=[TMP.b, sh1row.b], w=[HB.b])
                for k in range(8):
                    self.tr(pT0[:, k * 128:(k + 1) * 128], HB[:, k * 128:(k + 1) * 128], ident[:],
                            r=[HB.b, ident.b], w=[pT0.b])
                self.cp("act", HT[:].rearrange("p k t -> p (k t)"), pT0[:], r=[pT0.b], w=[HT.b])
                self.dma("dq1", sq.HT[:, :, tok], HT[:], r=[HT.b], w=[sq.HTb])
                if CUT <= 1:
                    continue
                for k in range(8):
                    self.mm(pF[0][:, 0:288], HT[:, k, :], Wc[:, k, C_CKV:C_CKV + 288], k == 0, k == 7,
                            r=[HT.b, Wc.b], w=[pF[0].b])
                for k in range(8):
                    self.mm(pF[1][:, 0:256], HT[:, k, :], Wc[:, k, C_SK:C_SK + 256], k == 0, k == 7,
                            r=[HT.b, Wc.b], w=[pF[1].b])
                for k in range(8):
                    self.mm(pF[4][:, 0:512], HT[:, k, :], Wc[:, k, C_SQ:C_SQ + 512], k == 0, k == 7,
                            r=[HT.b, Wc.b], w=[pF[4].b])
                if CUT <= 1.1:
                    continue
                self.act(junk[:, 0:256], pF[0][:, 0:256], AF.Square, r=[pF[0].b], w=[junk.b, ssk[b].b],
                         accum=ssk[b][:])
                self.rstd((k1[b], k2[b]), ssk[b], KV_RANK, rk[b])
                self.ts("dve", ckvn[b][:], pF[0][:, 0:256], rk[b][:], None, ALU.mult,
                        r=[pF[0].b, rk[b].b], w=[ckvn[b].b])
                if CUT <= 1.2:
                    continue
                for r_ in range(2):
                    self.tr(pT2[:, r_ * 128:(r_ + 1) * 128], ckvn[b][:, r_ * 128:(r_ + 1) * 128], ident[:],
                            r=[ckvn[b].b, ident.b], w=[pT2a])
                self.cp("act", ckvnT[b][:].rearrange("p k t -> p (k t)"), pT2[:, 0:256], r=[pT2a],
                        w=[ckvnT[b].b])
                if CUT <= 1.3:
                    continue
                for hh in range(2):
                    for r_ in range(2):
                        self.mm(pF[2 + hh][:, :], ckvnT[b][:, r_, :], Wkvg[:, r_, hh * 512:(hh + 1) * 512],
                                r_ == 0, r_ == 1, r=[ckvnT[b].b, Wkvg.b], w=[pF[2 + hh].b])
                if CUT <= 1.4:
                    continue
                for hh in range(2):
                    kvv = pF[2 + hh][:, :].rearrange("p (h d) -> p h d", d=128)
                    self.cp("dve", Kaug[b][:, hh * 4:(hh + 1) * 4, 0:64], kvv[:, :, 0:64], r=[pF[2 + hh].b],
                            w=[Kaug[b].b])
                    self.cp("dve", Vt[b][:, hh * 4:(hh + 1) * 4, 0:64], kvv[:, :, 64:128], r=[pF[2 + hh].b],
                            w=[Vt[b].b])
                if CUT <= 2:
                    continue
                if sq.rope:
                    xps = pF[0][:, 256:288].rearrange("p (h d) -> p h d", h=1)
                    t1, t2, t3, t4, v = rope(xps, pF[0].b, 1, 16, cosM, sinM, None, j)
                    kv_ = krr[b][:, :].rearrange("p (h d) -> p h d", h=1)
                    self.tt("pool", kv_[:, :, 0:16], v(t1), v(t2), ALU.subtract, r=[t1.b, t2.b], w=[krr[b].b])
                    self.tt("pool", kv_[:, :, 16:32], v(t3), v(t4), ALU.add, r=[t3.b, t4.b], w=[krr[b].b])
                else:
                    self.cp("dve", krr[b][:], pF[0][:, 256:288], r=[pF[0].b], w=[krr[b].b])
                self.cp("pool", Kaug[b][:, :, 64:96], bcast_mid(krr[b][:], 8), r=[krr[b].b], w=[Kaug[b].b])
                for h in range(8):
                    self.tr(pT1[0:96, h * 128:(h + 1) * 128], Kaug[b][:, h, :], ident[:],
                            r=[Kaug[b].b, ident.b], w=[pT1.b])
                self.cp("act", KTs[b][0:96].rearrange("p k t -> p (k t)"), pT1[0:96, :], r=[pT1.b], w=[KTs[b].b])
                key = slice(sq.key0 + j * 128, sq.key0 + (j + 1) * 128)
                self.dma("dq1", self.KT_mla[:, :, key], KTs[b][0:96], r=[KTs[b].b], w=[self.b_kv])
                self.dma("dq1", self.V_mla[key, :, :], Vt[b][:], r=[Vt[b].b], w=[self.b_kv])
                if CUT <= 3:
                    continue
                if sq.rope:
                    xps = pF[1][:, 0:128].rearrange("p (h d) -> p h d", d=64)
                    t1, t2, t3, t4, v = rope(xps, pF[1].b, 2, 32, cosS, sinS, None, j)
                    o = skr[b][:, :].rearrange("p (h d) -> p h d", d=64)
                    self.tt("pool", o[:, :, 0:32], v(t1), v(t2), ALU.subtract, r=[t1.b, t2.b], w=[skr[b].b])
                    self.tt("pool", o[:, :, 32:64], v(t3), v(t4), ALU.add, r=[t3.b, t4.b], w=[skr[b].b])
                else:
                    self.cp("dve", skr[b][:], pF[1][:, 0:128], r=[pF[1].b], w=[skr[b].b])
                self.cp("act", svt[b][:], pF[1][:, 128:256], r=[pF[1].b], w=[svt[b].b])
                self.tr(pT2[:, 256:384], skr[b][:], ident[:], r=[skr[b].b, ident.b], w=[pT2b])
                self.cp("act", KsT[b][:], pT2[:, 256:384], r=[pT2b], w=[KsT[b].b])
                self.dma("dq1", sq.KT_swa[:, tok], KsT[b][:], r=[KsT[b].b], w=[self.b_kv])
                self.dma("dq1", sq.V_swa[tok, :], svt[b][:], r=[svt[b].b], w=[self.b_kv])
                if CUT <= 4:
                    continue
                for (pp, c0, nh, h0) in ((pF[2], 0, 5, 0), (pF[3], 480, 3, 5)):
                    for k in range(8):
                        self.mm(pp[:, 0:nh * 96], HT[:, k, :], Wc[:, k, c0:c0 + nh * 96], k == 0, k == 7,
                                r=[HT.b, Wc.b], w=[pp.b])
                    qv = pp[:, 0:nh * 96].rearrange("p (h d) -> p h d", d=96)
                    qo = Qaug[b][:, h0:h0 + nh, :]
                    if sq.rope:
                        self.cp("dve", qo[:, :, 0:64], qv[:, :, 0:64], r=[pp.b], w=[Qaug[b].b])
                        t1, t2, t3, t4, v = rope(qv[:, :, 64:96], pp.b, nh, 16, cosM, sinM,
                                                 None, j)
                        vv = lambda t: t[:, 0:nh * 16].rearrange("p (h d) -> p h d", d=16)
                        self.tt("pool", qo[:, :, 64:80], vv(t1), vv(t2), ALU.subtract, r=[t1.b, t2.b],
                                w=[Qaug[b].b])
                        self.tt("pool", qo[:, :, 80:96], vv(t3), vv(t4), ALU.add, r=[t3.b, t4.b], w=[Qaug[b].b])
                    else:
                        self.cp("dve", qo, qv, r=[pp.b], w=[Qaug[b].b])
                for h in range(8):
                    self.tr(pT1[0:96, h * 128:(h + 1) * 128], Qaug[b][:, h, :], ident[:],
                            r=[Qaug[b].b, ident.b], w=[pT1.b])
                self.cp("dve", QTs[b][0:96].rearrange("p k t -> p (k t)"), pT1[0:96, :], r=[pT1.b], w=[QTs[b].b])
                self.dma("dq1", sq.QT_mla[:, :, tok], QTs[b][0:96], r=[QTs[b].b], w=[sq.Qb])
                if CUT <= 5:
                    continue
                o4 = sqr[b][:, :].rearrange("p (hh g d) -> p g hh d", hh=4, g=2)
                if sq.rope:
                    xps = pF[4][:, :].rearrange("p (h d) -> p h d", d=64)
                    t1, t2, t3, t4, v = rope(xps, pF[4].b, 8, 32, cosS, sinS, None, j)
                    v4 = lambda t: t[:, 0:256].rearrange("p (g hh d) -> p g hh d", g=2, hh=4)
                    self.tt("pool", o4[:, :, :, 0:32], v4(t1), v4(t2), ALU.subtract, r=[t1.b, t2.b], w=[sqr[b].b])
                    self.tt("pool", o4[:, :, :, 32:64], v4(t3), v4(t4), ALU.add, r=[t3.b, t4.b], w=[sqr[b].b])
                else:
                    self.cp("dve", o4, pF[4][:, :].rearrange("p (g hh d) -> p g hh d", g=2, hh=4),
                            r=[pF[4].b], w=[sqr[b].b])
                for hh in range(4):
                    self.tr(pT2[:, 384 + hh * 128:384 + (hh + 1) * 128], sqr[b][:, hh * 128:(hh + 1) * 128],
                            ident[:], r=[sqr[b].b, ident.b], w=[pT2c])
                self.cp("dve", QsT[b][:].rearrange("p k t -> p (k t)"), pT2[:, 384:896], r=[pT2c], w=[QsT[b].b])
                self.dma("dq1", sq.QT_swa[:, :, tok], QsT[b][:], r=[QsT[b].b], w=[sq.Qb])
            self.S.flush()

    def phase_attn(self, l, sq):
        L = sq.L
        lat = sq.rope
        NKq = (CTX + L) if lat else CTX
        nkt = NKq // 128
        CQ = min(512, L)
        nsub = CQ // 128
        with ExitStack() as ph:
            T = lambda *a, **k: self.T(ph, *a, **k)
            KT = T("KT", [128, 4, NKq], BF16)
            V = T("V", [128, nkt, 4, 65], BF16)
            QTc = [T("QTc", [128, 4, CQ], BF16) for _ in range(2)]
            PT = [T("PT", [128, CQ], BF16) for _ in range(3)]
            yo = [T("yo", [128, nsub, 64], BF16) for _ in range(2)]
            rden = [T("rden", [128, 4], F32) for _ in range(2)]
            pS = [T("pS", [128, 512], F32, psum=True) for _ in range(2)]
            pO = [T("pO", [128, 512], F32, psum=True) for _ in range(4)]
            it = 0
            for hg in range(2):
                self.dma("dq0", KT[0:96, :, :], self.KT_mla[:, hg * 4:(hg + 1) * 4, 0:NKq], r=[self.b_kv], w=[KT.b])
                for t0 in range(0, nkt, 8):
                    t1_ = min(nkt, t0 + 8)
                    self.dma("dq0", V[:, t0:t1_], self.V_mla[t0 * 128:t1_ * 128, hg * 4:(hg + 1) * 4, :]
                             .rearrange("(t p) h d -> p t h d", p=128), r=[self.b_kv], w=[V.b])
                for qc in range(L // CQ):
                    Q = QTc[qc % 2]
                    self.dma("dq0", Q[0:96, :, :], sq.QT_mla[:, hg * 4:(hg + 1) * 4, qc * CQ:(qc + 1) * CQ],
                             r=[sq.Qb], w=[Q.b])
                    for h in range(4):
                        for kt in range(nkt):
                            ps = pS[it % 2]
                            pt = PT[it % 3]
                            it += 1
                            self.mm(ps[:, 0:CQ], KT[0:96, h, kt * 128:(kt + 1) * 128], Q[0:96, h, :], True, True,
                                    r=[KT.b, Q.b], w=[ps.b])
                            self.act(pt[:], ps[:, 0:CQ], AF.Exp, r=[ps.b], w=[pt.b], scale=MLA_SCALE)
                            for s_ in range(nsub):
                                self.mm(pO[s_][:, 0:65], pt[:, s_ * 128:(s_ + 1) * 128], V[:, kt, h, :],
                                        kt == 0, kt == nkt - 1, r=[pt.b, V.b], w=[pO[s_].b])
                        Y = yo[h % 2]
                        R_ = rden[h % 2]
                        for s_ in range(nsub):
                            self.S.op("dve", lambda e, s_=s_, R_=R_: e.reciprocal(out=R_[:, s_:s_ + 1],
                                                                                   in_=pO[s_][:, 64:65]),
                                      r=[pO[s_].b], w=[R_.b])
                            self.ts("dve", Y[:, s_, :], pO[s_][:, 0:64], R_[:, s_:s_ + 1], None, ALU.mult,
                                    r=[pO[s_].b, R_.b], w=[Y.b])
                        hd = hg * 4 + h
                        self.dma("dq1", sq.YA[qc * CQ:(qc + 1) * CQ, hd * 64:(hd + 1) * 64]
                                 .rearrange("(s p) d -> p s d", p=128), Y[:], r=[Y.b], w=[sq.Yb])
            self.S.flush()
        nt = L // 128
        cs = self.seqs["ctx"]
        with ExitStack() as ph:
            T = lambda *a, **k: self.T(ph, *a, **k)
            Kc = T("Kc", [128, CTX], BF16)
            Vc = T("Vc", [128, 2, 2, 65], BF16)
            es = T("es", [128, 8], F32)
            self.dma("dq0", Kc[:], cs.KT_swa[:, :], r=[self.b_kv], w=[Kc.b])
            self.memset("dve", Vc[:, :, :, 64:65], 1.0, w=[Vc.b])
            for t_ in range(2):
                self.dma("dq0", Vc[:, t_, :, 0:64], cs.V_swa[t_ * 128:(t_ + 1) * 128, :]
                         .rearrange("p (g d) -> p g d", g=2), r=[self.b_kv], w=[Vc.b])
            self.dma("dq0", es[:], self.swa_sink[l].partition_broadcast(128), w=[es.b])
            self.act(es[:], es[:], AF.Exp, r=[es.b], w=[es.b])
            if lat:
                Kl = T("Kl", [128, L], BF16)
                Vl = T("Vl", [128, nt, 2, 65], BF16)
                triL = T("triL", [128, 128], BF16)
                triR = T("triR", [128, 128], BF16)
                self.dma("dq0", Kl[:], sq.KT_swa[:, :], r=[self.b_kv], w=[Kl.b])
                self.memset("dve", Vl[:, :, :, 64:65], 1.0, w=[Vl.b])
                for t_ in range(nt):
                    self.dma("dq0", Vl[:, t_, :, 0:64], sq.V_swa[t_ * 128:(t_ + 1) * 128, :]
                             .rearrange("p (g d) -> p g d", g=2), r=[self.b_kv], w=[Vl.b])
                self.dma("dq0", triL[:], self.c_triL[:, :], w=[triL.b])
                self.dma("dq0", triR[:], self.c_triR[:, :], w=[triR.b])
            qt = [T("qt", [128, 4, 128], BF16) for _ in range(2)]
            PT = [T("PTs", [128, 512], BF16) for _ in range(3)]
            yb = [T("yb", [128, 8, 64], BF16) for _ in range(2)]
            dn = [T("dn", [128, 8], F32) for _ in range(2)]
            rd = [T("rd", [128, 8], F32) for _ in range(2)]
            pS = [T("pSs", [128, 512], F32, psum=True) for _ in range(2)]
            pO = [T("pOs", [128, 512], F32, psum=True) for _ in range(4)]
            it = 0
            for i in range(nt):
                Q = qt[i % 2]
                Y = yb[i % 2]
                self.dma("dq0", Q[:], sq.QT_swa[:, :, i * 128:(i + 1) * 128], r=[sq.Qb], w=[Q.b])
                for g in range(2):
                    pr = slice(64 * g, 64 * g + 64)
                    tiles = [("c", 0, None), ("c", 1, None)]
                    if lat:
                        if i > 0:
                            tiles.append(("l", i - 1, triL))
                        tiles.append(("l", i, None))
                        if i < nt - 1:
                            tiles.append(("l", i + 1, triR))
                    for ti, (kind, kt, msk) in enumerate(tiles):
                        ps = pS[it % 2]
                        pt = PT[it % 3]
                        it += 1
                        Ksrc, Vsrc = (Kc, Vc) if kind == "c" else (Kl, Vl)
                        self.mm(ps[:, :], Ksrc[pr, kt * 128:(kt + 1) * 128],
                                Q[pr, :, :].rearrange("p h t -> p (h t)"), True, True, r=[Ksrc.b, Q.b], w=[ps.b])
                        self.act(pt[:], ps[:, :], AF.Exp, r=[ps.b], w=[pt.b], scale=SWA_SCALE)
                        if msk is not None:
                            pv = pt[:, :].rearrange("p (h t) -> p h t", h=4)
                            self.tt("pool", pv, pv, bcast_mid(msk[:, :], 4), ALU.mult, r=[pt.b, msk.b], w=[pt.b])
                        for hh in range(4):
                            self.mm(pO[hh][:, 0:65], pt[:, hh * 128:(hh + 1) * 128], Vsrc[:, kt, g, :],
                                    ti == 0, ti == len(tiles) - 1, r=[pt.b, Vsrc.b], w=[pO[hh].b])
                    for hh in range(4):
                        hd = g * 4 + hh
                        D_, R_ = dn[i % 2], rd[i % 2]
                        self.tt("dve", D_[:, hd:hd + 1], pO[hh][:, 64:65], es[:, hd:hd + 1], ALU.add,
                                r=[pO[hh].b, es.b], w=[D_.b])
                        self.S.op("dve", lambda e, D_=D_, R_=R_, hd=hd: e.reciprocal(out=R_[:, hd:hd + 1],
                                                                                    in_=D_[:, hd:hd + 1]),
                                  r=[D_.b], w=[R_.b])
                        self.ts("dve", Y[:, hd, :], pO[hh][:, 0:64], R_[:, hd:hd + 1], None, ALU.mult,
                                r=[pO[hh].b, R_.b], w=[Y.b])
                self.dma("dq1", sq.YB[i * 128:(i + 1) * 128, :], Y[:].rearrange("p h d -> p (h d)"),
                         r=[Y.b], w=[sq.Yb])
            self.S.flush()

    def ln_rows(self, ph, l, sq, i0, i1):
        a = self.T(ph, "arow", [128, D], F32)
        b = self.T(ph, "brow", [128, D], F32)
        self.dma("dq0", a[:], self.modrow(l, sq.set, i0), r=[self.b_modt], w=[a.b])
        self.dma("dq0", b[:], self.modrow(l, sq.set, i1), r=[self.b_modt], w=[b.b])
        return a, b

    def phase_merge(self, l, sq):
        L = sq.L
        CQ = min(512, L)
        nsub = CQ // 128
        with ExitStack() as ph:
            T = lambda *a, **k: self.T(ph, *a, **k)
            Wg = T("Wg", [128, 8, 3072], BF16)
            Wbr = T("Wbr", [128, 12, 1024], BF16)
            Wout = T("Wout", [128, 8, 1024], BF16)
            ident = T("ident", [128, 128], BF16)
            g1row = T("g1row", [128, D], F32)
            self.dma("dq0", ident[:], self.c_ident[:, :], w=[ident.b])
            self.dma("dq0", g1row[:], self.modrow(l, sq.set, 2), r=[self.b_modt], w=[g1row.b])
            with ExitStack() as ph2:
                stages = [self.T(ph2, "wstage", [128, 8, 512], F32) for _ in range(2)]
                self.load_w(ph2, Wg, self.w_in[l][:, C_GT:C_GT + 3072], 8, 3072, stages)
                for br in range(3):
                    self.load_w(ph2, Wbr[:, br * 4:(br + 1) * 4, :], self.w_branch[l, br], 4, 1024, stages,
                                dstb=Wbr.b)
                self.load_w(ph2, Wout, self.w_out[l], 8, 1024, stages)
                self.S.flush()
            hT = [T("hTc", [128, 8, CQ], BF16) for _ in range(2)]
            yin = [T("yin", [128, 512], BF16) for _ in range(2)]
            yT = [[T("yT", [128, 4, CQ], BF16) for _ in range(2)] for _ in range(3)]
            sig = [T("sig", [128, CQ], F32) for _ in range(2)]
            tmpm = [T("tmpm", [128, CQ], F32) for _ in range(2)]
            acc = [T("acc", [128, CQ], F32) for _ in range(2)]
            mT = [T("mT", [128, 8, CQ], BF16) for _ in range(2)]
            xt = [T("xt", [128, D], F32) for _ in range(2)]
            yn = [T("yn", [128, D], F32) for _ in range(2)]
            junk = T("junk", [128, 512], BF16)
            s2 = [T("ss2", [128, 2], F32) for _ in range(2)]
            ss, s1, s2_, rs = ([T(n, [128, 1], F32) for _ in range(2)] for n in ("ss", "s1", "s2", "rs"))
            pT = T("pT", [128, 1024], BF16, psum=True)
            pG = [T("pG", [128, 512], F32, psum=True) for _ in range(2)]
            pB = [T("pB", [128, 512], F32, psum=True) for _ in range(2)]
            pY = [T("pY", [128, 512], F32, psum=True) for _ in range(2)]
            it = 0
            for c in range(L // CQ):
                cb = c % 2
                tokc = slice(c * CQ, (c + 1) * CQ)
                H = hT[cb]
                self.dma("dq0", H[:], sq.HT[:, :, tokc], r=[sq.HTb], w=[H.b])
                for bi, src in enumerate((sq.YA, sq.YB)):
                    Yt = yT[bi][cb]
                    for s_ in range(nsub):
                        yi = yin[(2 * c + s_ + bi) % 2]
                        self.dma("dq0", yi[:], src[c * CQ + s_ * 128:c * CQ + (s_ + 1) * 128, :], r=[sq.Yb], w=[yi.b])
                        for kc in range(4):
                            self.tr(pT[:, kc * 128:(kc + 1) * 128], yi[:, kc * 128:(kc + 1) * 128], ident[:],
                                    r=[yi.b, ident.b], w=[pT.b])
                        self.cp("dve", Yt[:, :, s_ * 128:(s_ + 1) * 128],
                                pT[:, 0:512].rearrange("p (k t) -> p k t", k=4), r=[pT.b], w=[Yt.b])
                Yc = yT[2][cb]
                self.dma("dq0", Yc[:], sq.YCT[:, tokc].rearrange("(k p) t -> p k t", p=128), r=[sq.Ycb], w=[Yc.b])
                M = mT[cb]
                for ft in range(8):
                    A = acc[ft % 2]
                    for br in range(3):
                        pg, pb = pG[it % 2], pB[it % 2]
                        sg, tm = sig[it % 2], tmpm[it % 2]
                        it += 1
                        gc = (br * 8 + ft) * 128
                        for k in range(8):
                            self.mm(pg[:, 0:CQ], Wg[:, k, gc:gc + 128], H[:, k, :], k == 0, k == 7,
                                    r=[Wg.b, H.b], w=[pg.b])
                        Yt = yT[br][cb]
                        for kc in range(4):
                            self.mm(pb[:, 0:CQ], Wbr[:, br * 4 + kc, ft * 128:(ft + 1) * 128], Yt[:, kc, :],
                                    kc == 0, kc == 3, r=[Wbr.b, Yt.b], w=[pb.b])
                        self.act(sg[:], pg[:, 0:CQ], AF.Sigmoid, r=[pg.b], w=[sg.b])
                        if br == 0:
                            self.tt("dve", A[:], pb[:, 0:CQ], sg[:], ALU.mult, r=[pb.b, sg.b], w=[A.b])
                        else:
                            self.tt("dve", tm[:], pb[:, 0:CQ], sg[:], ALU.mult, r=[pb.b, sg.b], w=[tm.b])
                            if br == 1:
                                self.tt("pool", A[:], A[:], tm[:], ALU.add, r=[A.b, tm.b], w=[A.b])
                            else:
                                self.tt("pool", M[:, ft, :], A[:], tm[:], ALU.add, r=[A.b, tm.b], w=[M.b])
                for s_ in range(nsub):
                    j = c * nsub + s_
                    b = j % 2
                    tok = slice(j * 128, (j + 1) * 128)
                    X, YN = xt[b], yn[b]
                    self.dma("dq0", X[:], sq.x[tok, :], r=[sq.xb], w=[X.b])
                    for hf in range(2):
                        for k in range(8):
                            self.mm(pY[hf][:, :], M[:, k, s_ * 128:(s_ + 1) * 128], Wout[:, k, hf * 512:(hf + 1) * 512],
                                    k == 0, k == 7, r=[M.b, Wout.b], w=[pY[hf].b])
                    for hf in range(2):
                        self.act(junk[:], pY[hf][:, :], AF.Square, r=[pY[hf].b], w=[junk.b, s2[b].b],
                                 accum=s2[b][:, hf:hf + 1])
                    self.tt("dve", ss[b][:], s2[b][:, 0:1], s2[b][:, 1:2], ALU.add, r=[s2[b].b], w=[ss[b].b])
                    self.rstd((s1[b], s2_[b]), ss[b], D, rs[b])
                    for hf in range(2):
                        hs = slice(hf * 512, (hf + 1) * 512)
                        self.stt("dve", YN[:, hs], pY[hf][:, :], rs[b][:], g1row[:, hs], ALU.mult, ALU.mult,
                                 r=[pY[hf].b, rs[b].b, g1row.b], w=[YN.b])
                    self.tt("pool", YN[:], YN[:], X[:], ALU.add, r=[YN.b, X.b], w=[YN.b])
                    self.dma("dq1", sq.XMID[tok, :], YN[:], r=[YN.b], w=[sq.Xmb])
            self.S.flush()

    def phase_ln2(self, l, sq):
        L = sq.L
        nt = L // 128
        with ExitStack() as ph:
            T = lambda *a, **k: self.T(ph, *a, **k)
            ident = T("ident", [128, 128], BF16)
            self.dma("dq0", ident[:], self.c_ident[:, :], w=[ident.b])
            a2row, sh2row = self.ln_rows(ph, l, sq, 3, 4)
            xt = [T("xt", [128, D], F32) for _ in range(2)]
            tmp = [T("tmp", [128, D], F32) for _ in range(2)]
            hb = [T("hb", [128, D], BF16) for _ in range(2)]
            hT = [T("hT", [128, 8, 128], BF16) for _ in range(2)]
            junk = T("junk", [128, D], BF16)
            ss, s1, s2, rs = ([T(n, [128, 1], F32) for _ in range(2)] for n in ("ss", "s1", "s2", "rs"))
            pT0 = T("pT0", [128, 1024], BF16, psum=True)
            for j in range(nt):
                b = j % 2
                tok = slice(j * 128, (j + 1) * 128)
                X, TMP, HB, HT = xt[b], tmp[b], hb[b], hT[b]
                self.dma("dq0", X[:], sq.XMID[tok, :], r=[sq.Xmb], w=[X.b])
                self.act(junk[:], X[:], AF.Square, r=[X.b], w=[junk.b, ss[b].b], accum=ss[b][:])
                self.rstd((s1[b], s2[b]), ss[b], D, rs[b])
                self.stt("dve", TMP[:], X[:], rs[b][:], a2row[:], ALU.mult, ALU.mult,
                         r=[X.b, rs[b].b, a2row.b], w=[TMP.b])
                self.tt("pool", HB[:], TMP[:], sh2row[:], ALU.add, r=[TMP.b, sh2row.b], w=[HB.b])
                for k in range(8):
                    self.tr(pT0[:, k * 128:(k + 1) * 128], HB[:, k * 128:(k + 1) * 128], ident[:],
                            r=[HB.b, ident.b], w=[pT0.b])
                self.cp("act", HT[:].rearrange("p k t -> p (k t)"), pT0[:], r=[pT0.b], w=[HT.b])
                self.dma("dq1", sq.H2T[:, :, tok], HT[:], r=[HT.b], w=[sq.H2b])
            self.S.flush()

    def phase_ffn(self, l, sq, xout):
        L = sq.L
        C = 256
        NF = D_FF // 128
        with ExitStack() as ph:
            T = lambda *a, **k: self.T(ph, *a, **k)
            Wup = T("Wup", [128, 8, 2 * D_FF], BF16)
            Wdn = T("Wdn", [128, NF, D], BF16)
            cw = T("cw", [128, 44, 3], F32)
            cbias = T("cbias", [128, 44], F32)
            g2row = T("g2row", [128, D], F32)
            self.dma("dq0", cw[:], self.ffn_conv_w[l], w=[cw.b])
            self.dma("dq0", cbias[:], self.ffn_conv_b[l], w=[cbias.b])
            self.dma("dq0", g2row[:], self.modrow(l, sq.set, 5), r=[self.b_modt], w=[g2row.b])
            with ExitStack() as ph2:
                stages = [self.T(ph2, "wstage", [128, 22, 256], F32) for _ in range(2)]
                self.load_w(ph2, Wup, self.w_up[l], 8, 2 * D_FF, stages, cw=256)
                self.load_w(ph2, Wdn, self.w_down[l], NF, D, stages, cw=256)
                self.S.flush()
            h2 = [T("h2c", [128, 8, C + 2], BF16) for _ in range(2)]
            ua = [T("ua", [128, C], F32) for _ in range(2)]
            ub = [T("ub", [128, C], F32) for _ in range(2)]
            sa = [T("sa", [128, C], F32) for _ in range(2)]
            aT = [T("aT", [128, NF, C], BF16) for _ in range(2)]
            xt = [T("xt", [128, D], F32) for _ in range(2)]
            yn = [T("yn", [128, D], F32) for _ in range(2)]
            junk = T("junk", [128, 512], BF16)
            s2 = [T("ss2", [128, 2], F32) for _ in range(2)]
            ss, s1, s2_, rs = ([T(n, [128, 1], F32) for _ in range(2)] for n in ("ss", "s1", "s2", "rs"))
            pA = [T("pA", [128, 512], F32, psum=True) for _ in range(2)]
            pBb = [T("pBb", [128, 512], F32, psum=True) for _ in range(2)]
            pY = [T("pY", [128, 512], F32, psum=True) for _ in range(2)]
            it = 0
            for c in range(L // C):
                cb = c % 2
                c0 = c * C
                H = h2[cb]
                lo, hi = max(c0 - 1, 0), min(c0 + C + 1, L)
                if c0 == 0:
                    self.memset("dve", H[:, :, 0:1], 0.0, w=[H.b])
                if c0 + C == L:
                    self.memset("dve", H[:, :, C + 1:C + 2], 0.0, w=[H.b])
                self.dma("dq0", H[:, :, lo - (c0 - 1):hi - (c0 - 1)], sq.H2T[:, :, lo:hi], r=[sq.H2b], w=[H.b])
                A_T = aT[cb]
                for f in range(NF):
                    pa, pb = pA[it % 2], pBb[it % 2]
                    UA, UB, SA = ua[it % 2], ub[it % 2], sa[it % 2]
                    it += 1
                    for (pp, col0) in ((pa, f * 128), (pb, D_FF + f * 128)):
                        for k in range(8):
                            self.mm(pp[:, 0:C + 2], Wup[:, k, col0:col0 + 128], H[:, k, :], k == 0, k == 7,
                                    r=[Wup.b, H.b], w=[pp.b])
                    for (pp, U, fi) in ((pa, UA, f), (pb, UB, NF + f)):
                        self.act(U[:], pp[:, 1:C + 1], AF.Identity, r=[pp.b, cw.b, cbias.b], w=[U.b],
                                 scale=cw[:, fi, 1:2], bias=cbias[:, fi:fi + 1])
                        self.stt("dve", U[:], pp[:, 0:C], cw[:, fi, 0:1], U[:], ALU.mult, ALU.add,
                                 r=[pp.b, cw.b, U.b], w=[U.b])
                        self.stt("dve", U[:], pp[:, 2:C + 2], cw[:, fi, 2:3], U[:], ALU.mult, ALU.add,
                                 r=[pp.b, cw.b, U.b], w=[U.b])
                    self.act(SA[:], UA[:], AF.Silu, r=[UA.b], w=[SA.b])
                    self.tt("pool", A_T[:, f, :], SA[:], UB[:], ALU.mult, r=[SA.b, UB.b], w=[A_T.b])
                for s_ in range(C // 128):
                    j = c * (C // 128) + s_
                    b = j % 2
                    tok = slice(j * 128, (j + 1) * 128)
                    X, YN = xt[b], yn[b]
                    self.dma("dq0", X[:], sq.XMID[tok, :], r=[sq.Xmb], w=[X.b])
                    for hf in range(2):
                        for k in range(NF):
                            self.mm(pY[hf][:, :], A_T[:, k, s_ * 128:(s_ + 1) * 128], Wdn[:, k, hf * 512:(hf + 1) * 512],
                                    k == 0, k == NF - 1, r=[A_T.b, Wdn.b], w=[pY[hf].b])
                    for hf in range(2):
                        self.act(junk[:], pY[hf][:, :], AF.Square, r=[pY[hf].b], w=[junk.b, s2[b].b],
                                 accum=s2[b][:, hf:hf + 1])
                    self.tt("dve", ss[b][:], s2[b][:, 0:1], s2[b][:, 1:2], ALU.add, r=[s2[b].b], w=[ss[b].b])
                    self.rstd((s1[b], s2_[b]), ss[b], D, rs[b])
                    for hf in range(2):
                        hs = slice(hf * 512, (hf + 1) * 512)
                        self.stt("dve", YN[:, hs], pY[hf][:, :], rs[b][:], g2row[:, hs], ALU.mult, ALU.mult,
                                 r=[pY[hf].b, rs[b].b, g2row.b], w=[YN.b])
                    self.tt("pool", YN[:], YN[:], X[:], ALU.add, r=[YN.b, X.b], w=[YN.b])
                    self.dma("dq1", xout[tok, :], YN[:], r=[YN.b], w=[sq.xob])
            self.S.flush()

    def hy_tabs(self, L):
        if L not in self._hyc:
            t = fft_tables(L)
            h = hyena_consts(L)
            d = {}
            for k, v in list(t.items()) + list(h.items()):
                d[k] = self.const("c_%s_%d" % (k, L), v)
            self._hyc[L] = d
        return self._hyc[L]

    def fft_fwd(self, L, src, srcb, cb):
        H1 = L // 128
        KH = H1 + 1
        tb = self.hy_tabs(L)
        with ExitStack() as ph:
            T = lambda *a, **k: self.T(ph, *a, **k)
            xin = T("xin", [H1, 128, 128], BF16)
            F1 = T("F1", [H1, 2 * KH], BF16)
            A = T("A", [128, KH, 3, 128], BF16)
            G = T("G", [128, KH, 2, 128], BF16)
            pA = [T("pA", [128, 512], F32, psum=True) for _ in range(2)]
            pX = [T("pX", [128, 512], F32, psum=True) for _ in range(2)]
            for c8 in range(0, 128, 16):
                self.dma("dq0", xin[:, c8:c8 + 16, :], src[c8:c8 + 16, :].rearrange("c (a b) -> a c b", b=128),
                         r=[srcb], w=[xin.b])
            self.dma("dq0", F1[:], tb["F1"][:, :], w=[F1.b])
            self.dma("dq0", G[:], tb["G"][:, :, :, :], w=[G.b])
            nbm = max(1, 512 // (2 * KH))
            bi = 0
            for c0 in range(0, 128, nbm):
                nb = min(nbm, 128 - c0)
                p = pA[bi % 2]
                bi += 1
                for j in range(nb):
                    self.mm(p[:, j * 2 * KH:(j + 1) * 2 * KH], xin[0:H1, c0 + j, :], F1[0:H1, :], True, True,
                            r=[xin.b, F1.b], w=[p.b])
                pv = p[:, 0:nb * 2 * KH].rearrange("p (c k) -> p k c", k=2 * KH)
                self.cp("dve", A[:, :, 1, c0:c0 + nb], pv[:, 0:KH, :], r=[p.b], w=[A.b])
                self.cp("dve", A[:, :, 2, c0:c0 + nb], pv[:, KH:2 * KH, :], r=[p.b], w=[A.b])
                self.ts("dve", A[:, :, 0, c0:c0 + nb], pv[:, KH:2 * KH, :], -1.0, None, ALU.mult, r=[p.b], w=[A.b])
            bi = 0
            for k0 in range(0, KH, 2):
                nk = min(2, KH - k0)
                p = pX[bi % 2]
                bi += 1
                for kk in range(nk):
                    k1 = k0 + kk
                    o_ = p[:, kk * 256:(kk + 1) * 256]
                    self.mm(o_, G[:, k1, 0, :], A[:, k1, 1:3, :].rearrange("p a c -> p (a c)"), True, False,
                            r=[G.b, A.b], w=[p.b])
                    self.mm(o_, G[:, k1, 1, :], A[:, k1, 0:2, :].rearrange("p a c -> p (a c)"), False, True,
                            r=[G.b, A.b], w=[p.b])
                cb(k0, nk, p)
            self.S.flush()

    def fft_inv(self, L, Y, dst, dstb):
        H1 = L // 128
        KH = H1 + 1
        NCH = 64
        NB = 512 // NCH
        tb = self.hy_tabs(L)
        with ExitStack() as ph:
            T = lambda *a, **k: self.T(ph, *a, **k)
            Finv = T("Finv", [128, 3, 128], BF16)
            Pt = T("Pt", [KH, 128, 2, H1], BF16)
            nbuf = 2 if L <= 1024 else 1
            Ds = [T("Ds", [KH, 128, 2, NCH], BF16) for _ in range(nbuf)]
            yt = [T("yt", [H1, NCH, 128], F32) for _ in range(nbuf)]
            pC = [T("pC", [128, 512], F32, psum=True) for _ in range(2)]
            pY = [T("pYy", [128, 512], F32, psum=True) for _ in range(2)]
            self.dma("dq0", Finv[:], tb["Finv"][:, :, :], w=[Finv.b])
            self.dma("dq0", Pt[:], tb["P"][:, :, :, :], w=[Pt.b])
            bi = 0
            for sub in range(128 // NCH):
                D_ = Ds[sub % nbuf]
                YT = yt[sub % nbuf]
                for cl in range(0, NCH, 2):
                    p = pC[bi % 2]
                    bi += 1
                    for j in range(2):
                        ch = sub * NCH + cl + j
                        o_ = p[0:KH, j * 256:(j + 1) * 256]
                        self.mm(o_, Y[:, 0, ch, :], Finv[:, 1:3, :].rearrange("p a n -> p (a n)"), True, False,
                                r=[Y.b, Finv.b], w=[p.b])
                        self.mm(o_, Y[:, 1, ch, :], Finv[:, 0:2, :].rearrange("p a n -> p (a n)"), False, True,
                                r=[Y.b, Finv.b], w=[p.b])
                    self.cp("dve", D_[0:KH, :, :, cl:cl + 2],
                            p[0:KH, :].rearrange("p (c a n) -> p n a c", c=2, a=2), r=[p.b], w=[D_.b])
                for n0 in range(0, 128, NB):
                    p = pY[bi % 2]
                    bi += 1
                    for j in range(NB):
                        n2 = n0 + j
                        o_ = p[0:H1, j * NCH:(j + 1) * NCH]
                        self.mm(o_, Pt[0:KH, n2, 0, :], D_[0:KH, n2, 0, :], True, False, r=[Pt.b, D_.b], w=[p.b])
                        self.mm(o_, Pt[0:KH, n2, 1, :], D_[0:KH, n2, 1, :], False, True, r=[Pt.b, D_.b], w=[p.b])
                    self.cp("dve", YT[0:H1, :, n0:n0 + NB], p[0:H1, :].rearrange("p (n c) -> p c n", c=NCH),
                            r=[p.b], w=[YT.b])
                for c8 in range(0, NCH, 16):
                    self.dma("dq1", dst[sub * NCH + c8:sub * NCH + c8 + 16, :].rearrange("c (a b) -> a c b", b=128),
                             YT[0:H1, c8:c8 + 16, :], r=[YT.b], w=[dstb])
            self.S.flush()

    def phase_hy_proj(self, l, sq):
        L = sq.L
        C = 256
        with ExitStack() as ph:
            T = lambda *a, **k: self.T(ph, *a, **k)
            Why = T("Why", [128, 8, 1536], BF16)
            cw = T("cw", [128, 12, 3], F32)
            cbias = T("cbias", [128, 12], F32)
            self.dma("dq0", cw[:], self.hy_conv_w[l], w=[cw.b])
            self.dma("dq0", cbias[:], self.hy_conv_b[l], w=[cbias.b])
            with ExitStack() as ph2:
                stages = [self.T(ph2, "wstage", [128, 8, 512], F32) for _ in range(2)]
                self.load_w(ph2, Why, self.w_in[l][:, C_HY:C_HY + 1536], 8, 1536, stages)
                self.S.flush()
            Hc = [T("Hc", [128, 8, C + 2], BF16) for _ in range(2)]
            U = [T("U", [128, 12, C], F32) for _ in range(2)]
            Ub = [T("Ub", [128, 4, C], BF16) for _ in range(2)]
            pp_ = [T("pU", [128, 512], F32, psum=True) for _ in range(3)]
            it = 0
            for c in range(L // C):
                cb_ = c % 2
                c0 = c * C
                H = Hc[cb_]
                lo, hi = max(c0 - 1, 0), min(c0 + C + 1, L)
                if c0 == 0:
                    self.memset("dve", H[:, :, 0:1], 0.0, w=[H.b])
                if c0 + C == L:
                    self.memset("dve", H[:, :, C + 1:C + 2], 0.0, w=[H.b])
                self.dma("dq0", H[:, :, lo - (c0 - 1):hi - (c0 - 1)], sq.HT[:, :, lo:hi], r=[sq.HTb], w=[H.b])
                UU = U[cb_]
                for i in range(12):
                    pp = pp_[it % 3]
                    it += 1
                    for k in range(8):
                        self.mm(pp[:, 0:C + 2], Why[:, k, i * 128:(i + 1) * 128], H[:, k, :], k == 0, k == 7,
                                r=[Why.b, H.b], w=[pp.b])
                    self.act(UU[:, i, :], pp[:, 1:C + 1], AF.Identity, r=[pp.b, cw.b, cbias.b], w=[UU.b],
                             scale=cw[:, i, 1:2], bias=cbias[:, i:i + 1])
                    self.stt("dve", UU[:, i, :], pp[:, 0:C], cw[:, i, 0:1], UU[:, i, :], ALU.mult, ALU.add,
                             r=[pp.b, cw.b, UU.b], w=[UU.b])
                    self.stt("dve", UU[:, i, :], pp[:, 2:C + 2], cw[:, i, 2:3], UU[:, i, :], ALU.mult, ALU.add,
                             r=[pp.b, cw.b, UU.b], w=[UU.b])
                self.cp("pool", Ub[cb_][:], UU[:, 0:4, :], r=[UU.b], w=[Ub[cb_].b])
                self.dma("dq1", sq.UC[:, c0:c0 + C].rearrange("(i p) t -> p i t", p=128), UU[:], r=[UU.b], w=[sq.UCb])
                self.dma("dq1", sq.ZB[:, c0:c0 + C].rearrange("(i p) t -> p i t", p=128), Ub[cb_][:],
                         r=[Ub[cb_].b], w=[sq.ZBb])
            self.S.flush()

    def phase_hy_filters(self, l, sq):
        L = sq.L
        H1 = L // 128
        KH = H1 + 1
        CH = min(512, L)
        nch = L // CH
        tb = self.hy_tabs(L)
        PI = math.pi
        with ExitStack() as ph:
            T = lambda *a, **k: self.T(ph, *a, **k)
            hid2 = T("hid2", [64, L], F32)
            w3 = T("w3", [64, 2048], F32)
            self.dma("dq0", w3[:], self.hy_w3[l], w=[w3.b])
            with ExitStack() as ph1:
                T1 = lambda *a, **k: self.T(ph1, *a, **k)
                feats = T1("feats", [HY_EMB, L], F32)
                hid1 = T1("hid1", [64, L], F32)
                w1 = T1("w1", [HY_EMB, 64], F32)
                w2 = T1("w2", [64, 64], F32)
                fr = T1("fr", [64, 2], F32)
                bb = T1("bb", [64, 2], F32)
                fb = T1("fb", [64, 2], F32)
                arg = [T1("arg", [64, CH], F32) for _ in range(2)]
                m1 = [T1("m1", [64, CH], F32) for _ in range(2)]
                pm = [T1("pm", [128, 512], F32, psum=True) for _ in range(2)]
                self.dma("dq0", feats[:], tb["featsT"][:, :], w=[feats.b])
                self.dma("dq0", w1[:], self.hy_w1[l], w=[w1.b])
                self.dma("dq0", w2[:], self.hy_w2[l], w=[w2.b])
                self.dma("dq0", fr[:], self.hy_freq[l], w=[fr.b])
                self.dma("dq0", bb[:, 0:1], self.hy_b1[l], w=[bb.b])
                self.dma("dq0", bb[:, 1:2], self.hy_b2[l], w=[bb.b])
                self.tt("dve", fb[:], fr[:], bb[:], ALU.mult, r=[fr.b, bb.b], w=[fb.b])
                it = 0
                for li, (wt, KK, src, dst) in enumerate(((w1, HY_EMB, feats, hid1), (w2, 64, hid1, hid2))):
                    for c in range(nch):
                        cs = slice(c * CH, (c + 1) * CH)
                        p, a, m = pm[it % 2], arg[it % 2], m1[it % 2]
                        it += 1
                        self.mm(p[0:64, 0:CH], wt[0:KK, :], src[0:KK, cs], True, True, r=[wt.b, src.b], w=[p.b])
                        self.act(a[:], p[0:64, 0:CH], AF.Identity, r=[p.b, fr.b, fb.b], w=[a.b],
                                 scale=fr[:, li:li + 1], bias=fb[:, li:li + 1])
                        self.ts("dve", m[:], a[:], PI, -2 * PI, ALU.is_gt, ALU.mult, r=[a.b], w=[m.b])
                        self.tt("dve", a[:], a[:], m[:], ALU.add, r=[a.b, m.b], w=[a.b])
                        self.ts("dve", m[:], a[:], -PI, 2 * PI, ALU.is_lt, ALU.mult, r=[a.b], w=[m.b])
                        self.tt("dve", a[:], a[:], m[:], ALU.add, r=[a.b, m.b], w=[a.b])
                        self.act(dst[:, cs], a[:], AF.Sin, r=[a.b], w=[dst.b])
                self.S.flush()
            with ExitStack() as ph2:
                T2 = lambda *a, **k: self.T(ph2, *a, **k)
                hf = T2("hf", [128, L], F32)
                hb = T2("hb", [128, L], F32)
                so = [T2("so", [128, L], BF16) for _ in range(2)]
                dsc = T2("dsc", [128, 4], F32)
                dbias = T2("dbias", [128, 4, nch], F32)
                iota = T2("iota", [128, CH], F32)
                dec = [T2("dec", [128, CH], F32) for _ in range(2)]
                sm = [T2("sm", [128, 4], F32) for _ in range(2)]
                pw = [T2("pw", [128, 512], F32, psum=True) for _ in range(2)]
                self.dma("dq0", dsc[:], tb["dsc"][:, :], w=[dsc.b])
                self.dma("dq0", dbias[:], tb["dbias"][:, :, :], w=[dbias.b])
                self.dma("dq0", iota[:], tb["iota"][:, :], w=[iota.b])
                it = 0
                for g in range(4):
                    for o in range(2):
                        SM = sm[(g * 2 + o) % 2]
                        for di, hh_ in enumerate((hf, hb)):
                            col = di * 1024 + o * 512 + g * 128
                            for c in range(nch):
                                cs = slice(c * CH, (c + 1) * CH)
                                p, dc = pw[it % 2], dec[it % 2]
                                it += 1
                                self.mm(p[:, 0:CH], w3[0:64, col:col + 128], hid2[0:64, cs], True, True,
                                        r=[w3.b, hid2.b], w=[p.b])
                                self.act(dc[:], iota[:], AF.Exp, r=[iota.b, dsc.b, dbias.b], w=[dc.b],
                                         scale=dsc[:, g:g + 1], bias=dbias[:, g, c:c + 1])
                                self.tt("dve", hh_[:, cs], p[:, 0:CH], dc[:], ALU.mult, r=[p.b, dc.b], w=[hh_.b])
                        self.memset("dve", hb[:, 0:1], 0.0, w=[hb.b])
                        for di, hh_ in enumerate((hf, hb)):
                            self.S.op("dve", lambda e, hh_=hh_, di=di, SM=SM: e.tensor_reduce(
                                out=SM[:, di:di + 1], in_=hh_[:], axis=AX.X, op=ALU.add, apply_absolute_value=True),
                                r=[hh_.b], w=[SM.b])
                        self.tt("dve", SM[:, 2:3], SM[:, 0:1], SM[:, 1:2], ALU.add, r=[SM.b], w=[SM.b])
                        self.S.op("dve", lambda e, SM=SM: e.reciprocal(out=SM[:, 3:4], in_=SM[:, 2:3]),
                                  r=[SM.b], w=[SM.b])
                        for si, op_ in enumerate((ALU.add, ALU.subtract)):
                            O = so[si]
                            self.stt("dve" if si == 0 else "pool", O[:], hf[:], 0.0, hb[:], ALU.add, op_,
                                     r=[hf.b, hb.b], w=[O.b]) if si == 0 else \
                                self.tt("pool", O[:], hf[:], hb[:], op_, r=[hf.b, hb.b], w=[O.b])
                            self.ts("dve", O[:], O[:], SM[:, 3:4], None, ALU.mult, r=[O.b, SM.b], w=[O.b])
                            rows = slice(o * 512 + g * 128, o * 512 + (g + 1) * 128)
                            self.dma("dq1", sq.FT[si, rows, :], O[:], r=[O.b], w=[sq.FTb])
                self.S.flush()
        for o in range(2):
            for g in range(4):
                rows = slice(o * 512 + g * 128, o * 512 + (g + 1) * 128)
                with ExitStack() as ph:
                    Hs = self.T(ph, "Hs", [128, KH, 3, 128], BF16)

                    def cb_s(k0, nk, p, Hs=Hs):
                        pv = p[:, 0:nk * 256].rearrange("p (k a c) -> p k a c", a=2, c=128)
                        self.cp("dve", Hs[:, k0:k0 + nk, 1, :], pv[:, :, 0, :], r=[p.b], w=[Hs.b])

                    def cb_d(k0, nk, p, Hs=Hs):
                        pv = p[:, 0:nk * 256].rearrange("p (k a c) -> p k a c", a=2, c=128)
                        self.cp("dve", Hs[:, k0:k0 + nk, 2, :], pv[:, :, 1, :], r=[p.b], w=[Hs.b])
                        self.ts("dve", Hs[:, k0:k0 + nk, 0, :], pv[:, :, 1, :], -1.0, None, ALU.mult, r=[p.b],
                                w=[Hs.b])
                    self.fft_fwd(L, sq.FT[0, rows, :], sq.FTb, cb_s)
                    self.fft_fwd(L, sq.FT[1, rows, :], sq.FTb, cb_d)
                    self.dma("dq1", sq.HS[o, g], Hs[:], r=[Hs.b], w=[sq.HSb])
                    self.S.flush()

    def phase_hy_conv(self, l, sq):
        L = sq.L
        H1 = L // 128
        KH = H1 + 1
        CHK = min(2048, L)
        for o in range(2):
            for g in range(4):
                rows = slice(g * 128, (g + 1) * 128)
                with ExitStack() as ph:
                    T = lambda *a, **k: self.T(ph, *a, **k)
                    Y = T("Y", [128, 2, 128, KH], BF16)
                    with ExitStack() as phf:
                        Hb = [self.T(phf, "Hb", [128, 2, 3, 128], BF16) for _ in range(2)]
                        T1 = [self.T(phf, "T1", [128, 2, 2, 128], F32) for _ in range(2)]
                        T2 = [self.T(phf, "T2", [128, 2, 2, 128], F32) for _ in range(2)]
                        cnt = [0]

                        def cb(k0, nk, p, Y=Y, Hb=Hb, T1=T1, T2=T2, cnt=cnt, o=o, g=g):
                            i = cnt[0] % 2
                            cnt[0] += 1
                            hb_, t1, t2 = Hb[i], T1[i], T2[i]
                            self.dma("dq0", hb_[:, 0:nk], sq.HS[o, g][:, k0:k0 + nk], r=[sq.HSb], w=[hb_.b])
                            base = p[:, 0:nk * 256]
                            xr = bass.AP(base.tensor, base.offset, [list(base.ap[0]), [256, nk], [0, 2], [1, 128]])
                            xi = bass.AP(base.tensor, base.offset + 128,
                                         [list(base.ap[0]), [256, nk], [0, 2], [1, 128]])
                            self.tt("dve", t1[:, 0:nk], xr, hb_[:, 0:nk, 1:3, :], ALU.mult, r=[p.b, hb_.b], w=[t1.b])
                            self.tt("dve", t2[:, 0:nk], xi, hb_[:, 0:nk, 0:2, :], ALU.mult, r=[p.b, hb_.b], w=[t2.b])
                            self.tt("dve", Y[:, :, :, k0:k0 + nk].rearrange("p a c k -> p k a c"), t1[:, 0:nk],
                                    t2[:, 0:nk], ALU.add, r=[t1.b, t2.b], w=[Y.b])
                        self.fft_fwd(L, sq.ZB[rows, :], sq.ZBb, cb)
                    self.fft_inv(L, Y, sq.CV[rows, :], sq.CVb)
                with ExitStack() as ph:
                    T = lambda *a, **k: self.T(ph, *a, **k)
                    sk = T("sk", [128, 2, 4], F32)
                    self.dma("dq0", sk[:], self.hy_skip[l], w=[sk.b])
                    cv = [T("cv", [128, CHK], F32) for _ in range(2)]
                    zz = [T("zz", [128, CHK], F32) for _ in range(2)]
                    gt = [T("gt", [128, CHK], F32) for _ in range(2)]
                    zb = [T("zb", [128, CHK], BF16) for _ in range(2)]
                    zsrc = sq.UC if o == 0 else sq.Z2
                    zsb = sq.UCb if o == 0 else sq.Z2b
                    grow = slice(512 * (o + 1) + g * 128, 512 * (o + 1) + (g + 1) * 128)
                    for c in range(L // CHK):
                        i = c % 2
                        cs = slice(c * CHK, (c + 1) * CHK)
                        self.dma("dq0", cv[i][:], sq.CV[rows, cs], r=[sq.CVb], w=[cv[i].b])
                        self.dma("dq0", zz[i][:], zsrc[rows, cs], r=[zsb], w=[zz[i].b])
                        self.dma("dq0", gt[i][:], sq.UC[grow, cs], r=[sq.UCb], w=[gt[i].b])
                        self.stt("dve", zz[i][:], zz[i][:], sk[:, o, g:g + 1], cv[i][:], ALU.mult, ALU.add,
                                 r=[zz[i].b, sk.b, cv[i].b], w=[zz[i].b])
                        self.tt("pool", zz[i][:], zz[i][:], gt[i][:], ALU.mult, r=[zz[i].b, gt[i].b], w=[zz[i].b])
                        self.cp("dve", zb[i][:], zz[i][:], r=[zz[i].b], w=[zb[i].b])
                        if o == 0:
                            self.dma("dq1", sq.Z2[rows, cs], zz[i][:], r=[zz[i].b], w=[sq.Z2b])
                            self.dma("dq1", sq.ZB2[rows, cs], zb[i][:], r=[zb[i].b], w=[sq.ZBb])
                        else:
                            self.dma("dq1", sq.YCT[rows, cs], zb[i][:], r=[zb[i].b], w=[sq.Ycb])
                    self.S.flush()
            if o == 0:
                sq.ZB, sq.ZB2 = sq.ZB2, sq.ZB

    def phase_hyena(self, l, sq):
        self.phase_hy_proj(l, sq)
        self.phase_hy_filters(l, sq)
        self.phase_hy_conv(l, sq)

    def declare(self):
        L = self.L
        NK = self.NK
        dp = self.depth
        I = self.inp
        self.x = I("x", [L, D])
        self.ctx = I("ctx", [CTX, D])
        self.cT = I("cT", [128, 8, 2])
        self.w_mod = I("w_mod", [dp, D, 6 * D])
        self.b_mod = I("b_mod", [dp, 6 * D])
        self.norm_g = I("norm_g", [dp, 4, D])
        self.w_in = I("w_in", [dp, D, IN_W])
        self.kv_norm_g = I("kv_norm_g", [dp, 128, 2])
        self.w_kv_up = I("w_kv_up", [dp, KV_RANK, 1024])
        self.swa_sink = I("swa_sink", [dp, 8])
        self.hy_conv_w = I("hy_conv_w", [dp, 128, 12, 3])
        self.hy_conv_b = I("hy_conv_b", [dp, 128, 12])
        self.hy_w1 = I("hy_w1", [dp, HY_EMB, HY_HID])
        self.hy_b1 = I("hy_b1", [dp, HY_HID, 1])
        self.hy_w2 = I("hy_w2", [dp, HY_HID, HY_HID])
        self.hy_b2 = I("hy_b2", [dp, HY_HID, 1])
        self.hy_freq = I("hy_freq", [dp, HY_HID, 2])
        self.hy_w3 = I("hy_w3", [dp, HY_HID, 4 * HY_W])
        self.hy_skip = I("hy_skip", [dp, 128, 2, 4])
        self.w_branch = I("w_branch", [dp, 3, 512, D])
        self.w_out = I("w_out", [dp, D, D])
        self.w_up = I("w_up", [dp, D, 2 * D_FF])
        self.ffn_conv_w = I("ffn_conv_w", [dp, 128, 44, 3])
        self.ffn_conv_b = I("ffn_conv_b", [dp, 128, 44])
        self.w_down = I("w_down", [dp, D_FF, D])
        C = self.const
        self.c_ident = C("c_ident", _bf(np.eye(128)))
        self.c_identf = C("c_identf", _f32(np.eye(128)))
        pos = np.arange(L)
        cm, sm = rope_tables(pos, MLA_ROPE)
        cs, sn = rope_tables(pos, SWA_D)
        self.c_cosM = C("c_cosM", tile_major(cm))
        self.c_sinM = C("c_sinM", tile_major(sm))
        self.c_cosS = C("c_cosS", tile_major(cs))
        self.c_sinS = C("c_sinS", tile_major(sn))
        kk = np.arange(128)[:, None]
        qq = np.arange(128)[None, :]
        self.c_triL = C("c_triL", _bf((kk >= qq).astype(np.float32)))
        self.c_triR = C("c_triR", _bf((kk <= qq).astype(np.float32)))
        Sc = self.scr
        self.MODT = Sc("MODT", [dp, 2, 6, D])
        self.b_modt = Buf("modt")
        self.KT_mla = Sc("KT_mla", [96, 8, NK], BF16)
        self.V_mla = Sc("V_mla", [NK, 8, 65], BF16)
        self.b_kv = Buf("kv")
        self.seqs = {}
        for name, Ls, st, rope, key0 in (("ctx", CTX, 1, False, 0), ("lat", L, 0, True, CTX)):
            s = Seq()
            s.name, s.L, s.set, s.rope, s.key0 = name, Ls, st, rope, key0
            s.HT = Sc("HT_" + name, [128, 8, Ls], BF16)
            s.HTb = Buf()
            s.QT_mla = Sc("QT_mla_" + name, [96, 8, Ls], BF16)
            s.QT_swa = Sc("QT_swa_" + name, [128, 4, Ls], BF16)
            s.Qb = Buf()
            s.KT_swa = Sc("KT_swa_" + name, [128, Ls], BF16)
            s.V_swa = Sc("V_swa_" + name, [Ls, 128], BF16)
            s.xb = Buf()
            s.YA = Sc("YA_" + name, [Ls, 512], BF16)
            s.YB = Sc("YB_" + name, [Ls, 512], BF16)
            s.Yb = Buf()
            if "YCT_in" in self.dbg:
                s.YCT = self.inp("YCT_" + name, [512, Ls], BF16)
            else:
                s.YCT = Sc("YCT_" + name, [512, Ls], BF16)
            s.Ycb = Buf()
            s.XMID = Sc("XMID_" + name, [Ls, D])
            s.Xmb = Buf()
            s.UC = Sc("UC_" + name, [1536, Ls])
            s.UCb = Buf()
            s.ZB = Sc("ZB_" + name, [512, Ls], BF16)
            s.ZB2 = Sc("ZB2_" + name, [512, Ls], BF16)
            s.ZBb = Buf()
            s.Z2 = Sc("Z2_" + name, [512, Ls])
            s.Z2b = Buf()
            s.CV = Sc("CV_" + name, [512, Ls])
            s.CVb = Buf()
            s.FT = Sc("FT_" + name, [2, 1024, Ls], BF16)
            s.FTb = Buf()
            s.HS = Sc("HS_" + name, [2, 4, 128, Ls // 128 + 1, 3, 128], BF16)
            s.HSb = Buf()
            s.H2T = Sc("H2T_" + name, [128, 8, Ls], BF16)
            s.H2b = Buf()
            s.XO = Sc("XO_" + name, [Ls, D])
            s.xob = Buf()
            self.seqs[name] = s
        self.X1 = self.seqs["lat"].XO
        self.XC1 = self.seqs["ctx"].XO
        self.y = self.nc.dram_tensor("y", [self.NQ, D], F32, kind="ExternalOutput").ap()
        self.outs["y"] = self.y

    def build(self):
        nc = self.nc
        self.declare()
        with ExitStack() as es:
            self.S = Sched(nc, es)
            for l in range(self.depth):
                if self.layer(l):
                    break
        return nc

    def layer(self, l):
        lat, ctx = self.seqs["lat"], self.seqs["ctx"]
        lat.x = self.x if l == 0 else self.X1
        ctx.x = self.ctx if l == 0 else self.XC1
        self.phase_mod(l)
        if self.stop_after == "mod":
            return True
        self.phase_a1(l, ctx)
        self.phase_a1(l, lat)
        if self.stop_after == "a1":
            return True
        if "YCT_in" not in self.dbg:
            if l < self.depth - 1:
                self.phase_hyena(l, ctx)
            self.phase_hyena(l, lat)
            if self.stop_after == "hyena":
                return True
        self.phase_attn(l, ctx)
        self.phase_attn(l, lat)
        if self.stop_after == "attn":
            return True
        self.phase_merge(l, ctx)
        self.phase_merge(l, lat)
        if self.stop_after == "merge":
            return True
        last = (l == self.depth - 1)
        if not last:
            self.phase_ln2(l, ctx)
            self.phase_ffn(l, ctx, ctx.XO)
        self.phase_ln2(l, lat)
        self.phase_ffn(l, lat, self.y if (last and self.stop_after is None) else lat.XO)
        if self.stop_after == "ffn":
            return True
        return False


def host_inputs(inputs, b, L):
    g = lambda k: np.asarray(inputs[k], np.float32)
    dp = g("w_mod").shape[0]
    m = {}
    m["x"] = _f32(g("x")[b])
    m["ctx"] = _f32(g("ctx")[b])
    cT = np.stack([g("c")[b].reshape(8, 128).T, g("c_ctx").reshape(8, 128).T], axis=-1)
    m["cT"] = _f32(cT)
    for k in ("w_mod", "b_mod", "norm_g", "w_in", "w_kv_up", "swa_sink", "hy_w1", "hy_w2", "hy_w3",
              "w_branch", "w_out", "w_up", "w_down"):
        m[k] = _f32(g(k))
    m["kv_norm_g"] = _f32(g("kv_norm_g").reshape(dp, 2, 128).transpose(0, 2, 1))
    m["hy_conv_w"] = _f32(g("hy_conv_w").reshape(dp, 3, 12, 128).transpose(0, 3, 2, 1))
    m["hy_conv_b"] = _f32(g("hy_conv_b").reshape(dp, 12, 128).transpose(0, 2, 1))
    m["hy_b1"] = _f32(g("hy_b1").reshape(dp, HY_HID, 1))
    m["hy_b2"] = _f32(g("hy_b2").reshape(dp, HY_HID, 1))
    m["hy_freq"] = _f32(g("hy_freq").transpose(0, 2, 1))
    m["hy_skip"] = _f32(g("hy_skip").reshape(dp, 2, 4, 128).transpose(0, 3, 1, 2))
    m["ffn_conv_w"] = _f32(g("ffn_conv_w").reshape(dp, 3, 44, 128).transpose(0, 3, 2, 1))
    m["ffn_conv_b"] = _f32(g("ffn_conv_b").reshape(dp, 44, 128).transpose(0, 2, 1))
    return m


def kernel(**inputs):
    x = np.asarray(inputs["x"])
    B, L, _ = x.shape
    P = Prog(L, NH=1, depth=DEPTH, stop_after=None)
    nc = P.build()
    maps = []
    for core in range(8):
        m = host_inputs(inputs, core % B, L)
        m.update(P.consts)
        maps.append(m)
    res = run_bass_kernel_spmd(nc, maps, core_ids=list(range(8)))
    return np.stack([np.asarray(res.results[b]["y"], np.float32) for b in range(B)], axis=0)
```

```python
import math
import os
import numpy as np
import ml_dtypes
import concourse.bass as bass
import concourse.mybir as mybir
from concourse.bass_utils import run_bass_kernel_spmd
from contextlib import ExitStack

F32 = mybir.dt.float32
BF16 = mybir.dt.bfloat16
AF = mybir.ActivationFunctionType
ALU = mybir.AluOpType
AX = mybir.AxisListType

D = 1024
CTX = 256
GRID_W = 64
EPS = 1e-6
THETA = 10000.0
MLA_H, MLA_NOPE, MLA_ROPE, MLA_V, KV_RANK = 8, 64, 32, 64, 256
MLA_SCALE = (MLA_NOPE + MLA_ROPE) ** -0.5
SWA_H, SWA_KV, SWA_D = 8, 2, 64
SWA_SCALE = SWA_D ** -0.5
HY_W, HY_BANDS, HY_HID = 512, 16, 64
HY_EMB = 2 * HY_BANDS + 1
D_FF = 2816
IN_W = 6432
C_MQ, C_CKV, C_KR, C_SQ, C_SK, C_SV, C_HY, C_GT = 0, 768, 1024, 1056, 1568, 1696, 1824, 3360
DEPTH = 2

TRACKS = ["pe", "dve", "act", "pool", "dq0", "dq1"]
QOF = {"pe": "tensor", "dve": "vector", "act": "scalar", "pool": "gpsimd", "dq0": "sync", "dq1": "scalar"}
QUEUES = ["tensor", "vector", "scalar", "gpsimd", "sync"]
ISDMA = {"dq0": True, "dq1": True}
SAME_SYNC = {"pe": False, "dve": True, "act": True, "pool": True, "dq0": True, "dq1": True}


class Buf:
    __slots__ = ("name", "w", "r")

    def __init__(self, name=""):
        self.name = name
        self.w = None
        self.r = {}


NS = 8


class Sched:
    def __init__(self, nc, es):
        self.nc = nc
        self.sems = {}
        for t in TRACKS:
            if ISDMA.get(t, False):
                self.sems[t] = [es.enter_context(nc.semaphore("s_%s%d" % (t, i))) for i in range(NS)]
            else:
                self.sems[t] = es.enter_context(nc.semaphore("s_" + t))
        self.cnt = {t: 0 for t in TRACKS}
        self.seen = {q: {} for q in QUEUES}
        self.ops = {q: [] for q in QUEUES}
        self.nops = 0

    def _key(self, t, c):
        if ISDMA.get(t, False):
            slot = (c - 1) % NS
            use = (c - 1) // NS + 1
            return (t, slot), use, self.sems[t][slot], 16 * use
        return t, c, self.sems[t], c

    def _want(self, q, t, c, waits):
        key, lvl, sem, val = self._key(t, c)
        if self.seen[q].get(key, 0) < lvl:
            self.seen[q][key] = lvl
            waits.append((sem, val))

    def op(self, track, fn, r=(), w=()):
        q = QOF[track]
        need = {}
        dma = ISDMA.get(track, False)

        def req(dep):
            if dep is None:
                return
            t, c = dep
            if t == track and not SAME_SYNC[track]:
                return
            if ISDMA.get(t, False):
                need[(t, c)] = c
            elif c > need.get(t, 0):
                need[t] = c
        for b in r:
            req(b.w)
        for b in w:
            req(b.w)
            for (t, c) in b.r.values():
                if t != track or dma:
                    req((t, c))
        waits = []
        for k, c in need.items():
            t = k[0] if isinstance(k, tuple) else k
            self._want(q, t, c, waits)
        self.cnt[track] += 1
        c = self.cnt[track]
        if dma and c > NS:
            self._want(q, track, c - NS, waits)
        for b in r:
            if dma:
                b.r[(track, c)] = (track, c)
            else:
                b.r[track] = (track, c)
        for b in w:
            b.w = (track, c)
            b.r = {}
        if dma:
            _, _, sem, _ = self._key(track, c)
            inc = 16
        else:
            sem, inc = self.sems[track], 1
        self.ops[q].append((waits, fn, sem, inc))
        self.nops += 1

    def barrier(self):
        for q in QUEUES:
            waits = []
            for t in TRACKS:
                n = self.cnt[t]
                if n == 0:
                    continue
                if ISDMA.get(t, False):
                    for c in range(max(1, n - NS + 1), n + 1):
                        self._want(q, t, c, waits)
                else:
                    self._want(q, t, n, waits)
            if waits:
                self.ops[q].append((waits, None, None, None))

    def emit(self):
        with self.nc.Block() as block:
            for q in QUEUES:
                ops = self.ops[q]

                def body(engine, ops=ops):
                    for waits, fn, sem, inc in ops:
                        for (s_, v) in waits:
                            engine.wait_ge(s_, v)
                        if fn is not None:
                            fn(engine).then_inc(sem, inc)
                getattr(block, q)(body)
        self.ops = {q: [] for q in QUEUES}

    def flush(self):
        self.barrier()
        self.emit()


class TL:
    __slots__ = ("t", "b")

    def __init__(self, t, name=""):
        self.t = t
        self.b = Buf(name)

    def __getitem__(self, k):
        return self.t[k]


def bcast_mid(ap, n):
    sh = list(ap.shape)
    return ap.unsqueeze(1).to_broadcast([sh[0], n] + sh[1:])


def _bf(a):
    return np.ascontiguousarray(np.asarray(a, np.float32).astype(ml_dtypes.bfloat16))


def _f32(a):
    return np.ascontiguousarray(np.asarray(a, np.float32))


def rope_tables(pos, rot_dim):
    row = (pos // GRID_W).astype(np.float32)
    col = (pos % GRID_W).astype(np.float32)
    n_freq = rot_dim // 4
    inv = (np.float32(THETA) ** (-np.arange(n_freq, dtype=np.float32) / np.float32(n_freq))).astype(np.float32)
    ang = np.concatenate([row[:, None] * inv, col[:, None] * inv], axis=-1).astype(np.float32)
    return np.cos(ang).astype(np.float32), np.sin(ang).astype(np.float32)


def tile_major(tab):
    T, F = tab.shape
    return _f32(tab.reshape(T // 128, 128, F).transpose(1, 0, 2))


def fft_tables(L):
    H1 = L // 128
    N1 = 2 * H1
    N = 128 * N1
    KH = H1 + 1
    n1 = np.arange(H1)[:, None].astype(np.float64)
    k1 = np.arange(KH)[None, :].astype(np.float64)
    a = 2 * np.pi * n1 * k1 / N1
    F1 = np.concatenate([np.cos(a), -np.sin(a)], axis=1)
    n2 = np.arange(128).astype(np.float64)
    k2 = np.arange(128).astype(np.float64)
    kk = np.arange(KH).astype(np.float64)
    th = 2 * np.pi * n2[:, None, None] * (kk[None, :, None] + N1 * k2[None, None, :]) / N
    G = np.stack([np.cos(th), -np.sin(th)], axis=2)
    ph = 2 * np.pi * k2[:, None] * n2[None, :] / 128.0
    Finv = np.stack([-np.sin(ph), np.cos(ph), np.sin(ph)], axis=1)
    alpha = np.full(KH, 2.0)
    alpha[0] = 1.0
    alpha[H1] = 1.0
    nn1 = np.arange(H1).astype(np.float64)
    tp = 2 * np.pi * (128 * nn1[None, None, :] + n2[None, :, None]) * kk[:, None, None] / N
    P = np.stack([alpha[:, None, None] / N * np.cos(tp), -alpha[:, None, None] / N * np.sin(tp)], axis=2)
    return dict(F1=_bf(F1), G=_bf(G), Finv=_bf(Finv), P=_bf(P))


def hyena_consts(L):
    t_idx = np.arange(L, dtype=np.float32)
    t = t_idx / np.float32(max(L - 1, 1))
    bands = np.linspace(1e-4, HY_BANDS - 1, HY_BANDS, dtype=np.float32)
    ang = (np.float32(2.0 * math.pi / L) * t_idx[:, None] * bands).astype(np.float32)
    feats = np.concatenate([t[:, None], np.cos(ang), -np.sin(ang)], axis=-1).astype(np.float32)
    deltas = np.abs(np.linspace(math.log(1e-2) / 1.5, math.log(1e-2) / 0.3, HY_W, dtype=np.float32))
    dsc = (-deltas / np.float32(max(L - 1, 1))).astype(np.float32)
    dsc_t = dsc.reshape(4, 128).T
    CH = min(512, L)
    nch = L // CH
    dbias = dsc_t[:, :, None] * (np.arange(nch, dtype=np.float32) * CH)[None, None, :]
    iota = np.broadcast_to(np.arange(CH, dtype=np.float32)[None, :], (128, CH))
    return dict(featsT=_f32(feats.T), dsc=_f32(dsc_t), dbias=_f32(dbias), iota=_f32(iota))


class Seq:
    pass


class Prog:
    def __init__(self, L, NH=1, depth=DEPTH, dbg=(), stop_after=None):
        self.L = L
        self.NH = NH
        self.NQ = L // NH
        self.depth = depth
        self.dbg = set(dbg)
        self.stop_after = stop_after
        self.nc = bass.Bass("TRN2", target_bir_lowering=False)
        self.consts = {}
        self._hyc = {}
        self.ins = {}
        self.outs = {}
        self.NK = CTX + L

    def inp(self, name, shape, dt=F32):
        ap = self.nc.dram_tensor(name, list(shape), dt, kind="ExternalInput").ap()
        self.ins[name] = ap
        return ap

    def const(self, name, arr):
        dt = BF16 if arr.dtype == ml_dtypes.bfloat16 else F32
        self.consts[name] = arr
        return self.inp(name, arr.shape, dt)

    def scr(self, name, shape, dt=F32):
        kind = "ExternalOutput" if name in self.dbg else "Internal"
        ap = self.nc.dram_tensor(name, list(shape), dt, kind=kind).ap()
        if name in self.dbg:
            self.outs[name] = ap
        return ap

    def T(self, ph, name, shape, dt, psum=False):
        self._tn = getattr(self, "_tn", 0) + 1
        nm = f"{name}_{self._tn}"
        if psum:
            t = ph.enter_context(self.nc.psum_tensor(nm, list(shape), dt))
        else:
            t = ph.enter_context(self.nc.sbuf_tensor(nm, list(shape), dt))
        return TL(t, nm)

    def dma(self, track, out_ap, in_ap, r=(), w=()):
        self.S.op(track, lambda e: e.dma_start(out=out_ap, in_=in_ap), r=r, w=w)

    def tt(self, track, out_ap, in0, in1, op, r=(), w=()):
        self.S.op(track, lambda e: e.tensor_tensor(out=out_ap, in0=in0, in1=in1, op=op), r=r, w=w)

    def ts(self, track, out_ap, in0, s1, s2, op0, op1=None, r=(), w=(), accum=None):
        if op1 is None:
            self.S.op(track, lambda e: e.tensor_scalar(out=out_ap, in0=in0, scalar1=s1, scalar2=None, op0=op0),
                      r=r, w=w)
        else:
            self.S.op(track, lambda e: e.tensor_scalar(out=out_ap, in0=in0, scalar1=s1, scalar2=s2, op0=op0,
                                                         op1=op1), r=r, w=w)

    def stt(self, track, out_ap, in0, scalar, in1, op0, op1, r=(), w=()):
        self.S.op(track, lambda e: e.scalar_tensor_tensor(out=out_ap, in0=in0, scalar=scalar, in1=in1,
                                                           op0=op0, op1=op1), r=r, w=w)

    def cp(self, track, out_ap, in_ap, r=(), w=()):
        if track == "act":
            self.S.op(track, lambda e: e.copy(out=out_ap, in_=in_ap), r=r, w=w)
        else:
            self.S.op(track, lambda e: e.tensor_copy(out=out_ap, in_=in_ap), r=r, w=w)

    def act(self, out_ap, in_ap, func, r=(), w=(), bias=None, scale=None, accum=None):
        kw = {}
        if bias is not None:
            kw["bias"] = bias
        if scale is not None:
            kw["scale"] = scale
        if accum is not None:
            kw["accum_out"] = accum
        self.S.op("act", lambda e: e.activation(out=out_ap, in_=in_ap, func=func, **kw), r=r, w=w)

    def mm(self, out_ap, lhsT, rhs, start, stop, r=(), w=()):
        self.S.op("pe", lambda e: e.matmul(out_ap, lhsT=lhsT, rhs=rhs, start=start, stop=stop), r=r, w=w)

    def tr(self, out_ap, in_ap, ident, r=(), w=()):
        self.S.op("pe", lambda e: e.transpose(out_ap, in_ap, ident), r=r, w=w)

    def memset(self, track, ap, val, w=()):
        self.S.op(track, lambda e: e.memset(ap, val), w=w)

    def rstd(self, ph_tiles, ss, n, out):
        t1, t2 = ph_tiles
        self.ts("dve", t1[:], ss[:], 1.0 / n, EPS, ALU.mult, ALU.add, r=[ss.b], w=[t1.b])
        self.act(t2[:], t1[:], AF.Sqrt, r=[t1.b], w=[t2.b])
        self.S.op("dve", lambda e: e.reciprocal(out=out[:], in_=t2[:]), r=[t2.b], w=[out.b])

    def load_w(self, ph, dst, src, KC, ncols, stages, cw=None, dstb=None):
        if cw is None:
            cw = max(64, min(512, (4096 // KC) // 64 * 64))
        i = 0
        for c0 in range(0, ncols, cw):
            c1 = min(ncols, c0 + cw)
            st = stages[i % len(stages)]
            self.dma("dq0", st[:, 0:KC, 0:c1 - c0], src[:, c0:c1].rearrange("(k p) c -> p k c", p=128),
                     w=[st.b])
            trk = "dve" if i % 2 == 0 else "pool"
            self.cp(trk, dst[:, :, c0:c1], st[:, 0:KC, 0:c1 - c0], r=[st.b], w=[dstb if dstb is not None else dst.b])
            i += 1

    def phase_mod(self, l):
        with ExitStack() as ph:
            T = lambda *a, **k: self.T(ph, *a, **k)
            sil = T("sil", [128, 8, 2], F32)
            brow = T("brow", [2, 6 * D], F32)
            ng = T("ng", [2, 4, D], F32)
            mod = T("mod", [2, 6 * D], F32)
            tab = T("tab", [2, 6, D], F32)
            wst = [T("wst", [128, 8, 512], F32) for _ in range(3)]
            ps = [T("psm", [128, 512], F32, psum=True) for _ in range(2)]
            self.dma("dq0", sil[:], self.cT[:, :, :], w=[sil.b])
            self.dma("dq0", brow[:], self.b_mod[l].partition_broadcast(2), w=[brow.b])
            self.dma("dq0", ng[:], self.norm_g[l].rearrange("a d -> (a d)").partition_broadcast(2)
                     .rearrange("p (a d) -> p a d", a=4), w=[ng.b])
            self.act(sil[:], sil[:], AF.Silu, r=[sil.b], w=[sil.b])
            for j in range(12):
                st = wst[j % 3]
                self.dma("dq0", st[:], self.w_mod[l][:, j * 512:(j + 1) * 512]
                         .rearrange("(k p) c -> p k c", p=128), w=[st.b])
                p = ps[j % 2]
                for k in range(8):
                    self.mm(p[0:2, :], sil[:, k, :], st[:, k, :], k == 0, k == 7, r=[sil.b, st.b], w=[p.b])
                self.tt("dve", mod[:, j * 512:(j + 1) * 512], p[0:2, :], brow[:, j * 512:(j + 1) * 512],
                        ALU.add, r=[p.b, brow.b], w=[mod.b])
            m = lambda i: mod[:, i * D:(i + 1) * D]
            self.stt("dve", tab[:, 0, :], m(1), 1.0, ng[:, 0, :], ALU.add, ALU.mult, r=[mod.b, ng.b], w=[tab.b])
            self.cp("dve", tab[:, 1, :], m(0), r=[mod.b], w=[tab.b])
            self.tt("dve", tab[:, 2, :], m(2), ng[:, 1, :], ALU.mult, r=[mod.b, ng.b], w=[tab.b])
            self.stt("dve", tab[:, 3, :], m(4), 1.0, ng[:, 2, :], ALU.add, ALU.mult, r=[mod.b, ng.b], w=[tab.b])
            self.cp("dve", tab[:, 4, :], m(3), r=[mod.b], w=[tab.b])
            self.tt("dve", tab[:, 5, :], m(5), ng[:, 3, :], ALU.mult, r=[mod.b, ng.b], w=[tab.b])
            self.dma("dq0", self.MODT[l], tab[:], r=[tab.b], w=[self.b_modt])
            self.S.flush()

    def modrow(self, l, s, i):
        return self.MODT[l][s, i, :].partition_broadcast(128)

    def phase_a1(self, l, sq):
        L = sq.L
        nt = L // 128
        with ExitStack() as ph:
            T = lambda *a, **k: self.T(ph, *a, **k)
            dbl = lambda *a, **k: [self.T(ph, *a, **k) for _ in range(2)]
            Wc = T("Wc", [128, 8, 1824], BF16)
            Wkvg = T("Wkvg", [128, 2, 1024], BF16)
            kvg = T("kvg", [128, 2], F32)
            ident = T("ident", [128, 128], BF16)
            a1row = T("a1row", [128, D], F32)
            sh1row = T("sh1row", [128, D], F32)
            self.dma("dq0", ident[:], self.c_ident[:, :], w=[ident.b])
            self.dma("dq0", a1row[:], self.modrow(l, sq.set, 0), r=[self.b_modt], w=[a1row.b])
            self.dma("dq0", sh1row[:], self.modrow(l, sq.set, 1), r=[self.b_modt], w=[sh1row.b])
            self.dma("dq0", kvg[:], self.kv_norm_g[l], w=[kvg.b])
            with ExitStack() as ph2:
                stages = [self.T(ph2, "wstage", [128, 8, 512], F32) for _ in range(2)]
                self.load_w(ph2, Wc, self.w_in[l][:, 0:1824], 8, 1824, stages)
                st = stages[0]
                self.dma("dq0", st[:, 0:2, 0:512], self.w_kv_up[l][:, 0:512].rearrange("(k p) c -> p k c", p=128),
                         w=[st.b])
                st1 = stages[1]
                self.dma("dq0", st1[:, 0:2, 0:512], self.w_kv_up[l][:, 512:1024]
                         .rearrange("(k p) c -> p k c", p=128), w=[st1.b])
                for k in range(2):
                    self.ts("dve", Wkvg[:, k, 0:512], st[:, k, 0:512], kvg[:, k:k + 1], None, ALU.mult,
                            r=[st.b, kvg.b], w=[Wkvg.b])
                    self.ts("dve", Wkvg[:, k, 512:1024], st1[:, k, 0:512], kvg[:, k:k + 1], None, ALU.mult,
                            r=[st1.b, kvg.b], w=[Wkvg.b])
                self.S.flush()
            if sq.rope:
                cosM = T("cosM", [128, nt, 16], F32)
                sinM = T("sinM", [128, nt, 16], F32)
                cosS = T("cosS", [128, nt, 32], F32)
                sinS = T("sinS", [128, nt, 32], F32)
                for tl, src in ((cosM, self.c_cosM), (sinM, self.c_sinM), (cosS, self.c_cosS), (sinS, self.c_sinS)):
                    self.dma("dq0", tl[:], src[:, :, :], w=[tl.b])
            xt = dbl("xt", [128, D], F32)
            junk = T("junk", [128, D], BF16)
            tmp = dbl("tmp", [128, D], F32)
            hb = dbl("hb", [128, D], BF16)
            hT = dbl("hT", [128, 8, 128], BF16)
            st_ = lambda n: dbl(n, [128, 1], F32)
            ss, s1, s2, rs = st_("ss"), st_("s1"), st_("s2"), st_("rs")
            ssk, k1, k2, rk = st_("ssk"), st_("k1"), st_("k2"), st_("rk")
            ckvn = dbl("ckvn", [128, 256], BF16)
            ckvnT = dbl("ckvnT", [128, 2, 128], BF16)
            Kaug = dbl("Kaug", [128, 8, 96], BF16)
            Vt = dbl("Vt", [128, 8, 65], BF16)
            krr = dbl("krr", [128, 32], F32)
            rt = [dbl("rt%d" % i, [128, 256], F32) for i in range(4)]
            KTs = dbl("KTs", [128, 8, 128], BF16)
            skr = dbl("skr", [128, 128], BF16)
            KsT = dbl("KsT", [128, 128], BF16)
            svt = dbl("svt", [128, 128], BF16)
            Qaug = dbl("Qaug", [128, 8, 96], BF16)
            QTs = dbl("QTs", [128, 8, 128], BF16)
            sqr = dbl("sqr", [128, 512], BF16)
            QsT = dbl("QsT", [128, 4, 128], BF16)
            pT0 = T("pT0", [128, 1024], BF16, psum=True)
            pT1 = T("pT1", [128, 1024], BF16, psum=True)
            pT2 = T("pT2", [128, 1024], BF16, psum=True)
            pF = [T("pF%d" % i, [128, 512], F32, psum=True) for i in range(5)]
            pT2a = pT2b = pT2c = pT2.b

            def rope(xps, psb, nh, half, cos_tl, sin_tl, out_ap, j):
                t1, t2, t3, t4 = (rt[i][j % 2] for i in range(4))
                n = nh * half
                v = lambda t: t[:, 0:n].rearrange("p (h d) -> p h d", d=half)
                x1 = xps[:, :, 0:half]
                x2 = xps[:, :, half:2 * half]
                cb = bcast_mid(cos_tl[:, j, :], nh)
                sb = bcast_mid(sin_tl[:, j, :], nh)
                rb = [psb, cos_tl.b, sin_tl.b]
                self.tt("dve", v(t1), x1, cb, ALU.mult, r=rb, w=[t1.b])
                self.tt("dve", v(t2), x2, sb, ALU.mult, r=rb, w=[t2.b])
                self.tt("dve", v(t3), x2, cb, ALU.mult, r=rb, w=[t3.b])
                self.tt("dve", v(t4), x1, sb, ALU.mult, r=rb, w=[t4.b])
                return (t1, t2, t3, t4, v)

            for b_ in range(2):
                self.memset("dve", Vt[b_][:, :, 64:65], 1.0, w=[Vt[b_].b])
            CUT = float(os.environ.get("A1CUT", "9"))
            for j in range(nt):
                b = j % 2
                tok = slice(j * 128, (j + 1) * 128)
                X, TMP, HB, HT = xt[b], tmp[b], hb[b], hT[b]
                self.dma("dq0", X[:], sq.x[tok, :], r=[sq.xb], w=[X.b])
                self.act(junk[:], X[:], AF.Square, r=[X.b], w=[junk.b, ss[b].b], accum=ss[b][:])
                self.rstd((s1[b], s2[b]), ss[b], D, rs[b])
                self.stt("dve", TMP[:], X[:], rs[b][:], a1row[:], ALU.mult, ALU.mult,
                         r=[X.b, rs[b].b, a1row.b], w=[TMP.b])
                self.tt("pool", HB[:], TMP[:], sh1row[:], ALU.add, r=[TMP.b, sh1row.b], w=[HB.b])
                for k in range(8):
                    self.tr(pT0[:, k * 128:(k + 1) * 128], HB[:, k * 128:(k + 1) * 128], ident[:],
                            r=[HB.b, ident.b], w=[pT0.b])
                self.cp("act", HT[:].rearrange("p k t -> p (k t)"), pT0[:], r=[pT0.b], w=[HT.b])
                self.dma("dq1", sq.HT[:, :, tok], HT[:], r=[HT.b], w=[sq.HTb])
                if CUT <= 1:
                    continue
                for k in range(8):
                    self.mm(pF[0][:, 0:288], HT[:, k, :], Wc[:, k, C_CKV:C_CKV + 288], k == 0, k == 7,
                            r=[HT.b, Wc.b], w=[pF[0].b])
                for k in range(8):
                    self.mm(pF[1][:, 0:256], HT[:, k, :], Wc[:, k, C_SK:C_SK + 256], k == 0, k == 7,
                            r=[HT.b, Wc.b], w=[pF[1].b])
                for k in range(8):
                    self.mm(pF[4][:, 0:512], HT[:, k, :], Wc[:, k, C_SQ:C_SQ + 512], k == 0, k == 7,
                            r=[HT.b, Wc.b], w=[pF[4].b])
                if CUT <= 1.1:
                    continue
                self.act(junk[:, 0:256], pF[0][:, 0:256], AF.Square, r=[pF[0].b], w=[junk.b, ssk[b].b],
                         accum=ssk[b][:])
                self.rstd((k1[b], k2[b]), ssk[b], KV_RANK, rk[b])
                self.ts("dve", ckvn[b][:], pF[0][:, 0:256], rk[b][:], None, ALU.mult,
                        r=[pF[0].b, rk[b].b], w=[ckvn[b].b])
                if CUT <= 1.2:
                    continue
                for r_ in range(2):
                    self.tr(pT2[:, r_ * 128:(r_ + 1) * 128], ckvn[b][:, r_ * 128:(r_ + 1) * 128], ident[:],
                            r=[ckvn[b].b, ident.b], w=[pT2a])
                self.cp("act", ckvnT[b][:].rearrange("p k t -> p (k t)"), pT2[:, 0:256], r=[pT2a],
                        w=[ckvnT[b].b])
                if CUT <= 1.3:
                    continue
                for hh in range(2):
                    for r_ in range(2):
                        self.mm(pF[2 + hh][:, :], ckvnT[b][:, r_, :], Wkvg[:, r_, hh * 512:(hh + 1) * 512],
                                r_ == 0, r_ == 1, r=[ckvnT[b].b, Wkvg.b], w=[pF[2 + hh].b])
                if CUT <= 1.4:
                    continue
                for hh in range(2):
                    kvv = pF[2 + hh][:, :].rearrange("p (h d) -> p h d", d=128)
                    self.cp("dve", Kaug[b][:, hh * 4:(hh + 1) * 4, 0:64], kvv[:, :, 0:64], r=[pF[2 + hh].b],
                            w=[Kaug[b].b])
                    self.cp("dve", Vt[b][:, hh * 4:(hh + 1) * 4, 0:64], kvv[:, :, 64:128], r=[pF[2 + hh].b],
                            w=[Vt[b].b])
                if CUT <= 2:
                    continue
                if sq.rope:
                    xps = pF[0][:, 256:288].rearrange("p (h d) -> p h d", h=1)
                    t1, t2, t3, t4, v = rope(xps, pF[0].b, 1, 16, cosM, sinM, None, j)
                    kv_ = krr[b][:, :].rearrange("p (h d) -> p h d", h=1)
                    self.tt("pool", kv_[:, :, 0:16], v(t1), v(t2), ALU.subtract, r=[t1.b, t2.b], w=[krr[b].b])
                    self.tt("pool", kv_[:, :, 16:32], v(t3), v(t4), ALU.add, r=[t3.b, t4.b], w=[krr[b].b])
                else:
                    self.cp("dve", krr[b][:], pF[0][:, 256:288], r=[pF[0].b], w=[krr[b].b])
                self.cp("pool", Kaug[b][:, :, 64:96], bcast_mid(krr[b][:], 8), r=[krr[b].b], w=[Kaug[b].b])
                for h in range(8):
                    self.tr(pT1[0:96, h * 128:(h + 1) * 128], Kaug[b][:, h, :], ident[:],
                            r=[Kaug[b].b, ident.b], w=[pT1.b])
                self.cp("act", KTs[b][0:96].rearrange("p k t -> p (k t)"), pT1[0:96, :], r=[pT1.b], w=[KTs[b].b])
                key = slice(sq.key0 + j * 128, sq.key0 + (j + 1) * 128)
                self.dma("dq1", self.KT_mla[:, :, key], KTs[b][0:96], r=[KTs[b].b], w=[self.b_kv])
                self.dma("dq1", self.V_mla[key, :, :], Vt[b][:], r=[Vt[b].b], w=[self.b_kv])
                if CUT <= 3:
                    continue
                if sq.rope:
                    xps = pF[1][:, 0:128].rearrange("p (h d) -> p h d", d=64)
                    t1, t2, t3, t4, v = rope(xps, pF[1].b, 2, 32, cosS, sinS, None, j)
                    o = skr[b][:, :].rearrange("p (h d) -> p h d", d=64)
                    self.tt("pool", o[:, :, 0:32], v(t1), v(t2), ALU.subtract, r=[t1.b, t2.b], w=[skr[b].b])
                    self.tt("pool", o[:, :, 32:64], v(t3), v(t4), ALU.add, r=[t3.b, t4.b], w=[skr[b].b])
                else:
                    self.cp("dve", skr[b][:], pF[1][:, 0:128], r=[pF[1].b], w=[skr[b].b])
                self.cp("act", svt[b][:], pF[1][:, 128:256], r=[pF[1].b], w=[svt[b].b])
                self.tr(pT2[:, 256:384], skr[b][:], ident[:], r=[skr[b].b, ident.b], w=[pT2b])
                self.cp("act", KsT[b][:], pT2[:, 256:384], r=[pT2b], w=[KsT[b].b])
                self.dma("dq1", sq.KT_swa[:, tok], KsT[b][:], r=[KsT[b].b], w=[self.b_kv])
                self.dma("dq1", sq.V_swa[tok, :], svt[b][:], r=[svt[b].b], w=[self.b_kv])
                if CUT <= 4:
                    continue
                for (pp, c0, nh, h0) in ((pF[2], 0, 5, 0), (pF[3], 480, 3, 5)):
                    for k in range(8):
                        self.mm(pp[:, 0:nh * 96], HT[:, k, :], Wc[:, k, c0:c0 + nh * 96], k == 0, k == 7,
                                r=[HT.b, Wc.b], w=[pp.b])
                    qv = pp[:, 0:nh * 96].rearrange("p (h d) -> p h d", d=96)
                    qo = Qaug[b][:, h0:h0 + nh, :]
                    if sq.rope:
                        self.cp("dve", qo[:, :, 0:64], qv[:, :, 0:64], r=[pp.b], w=[Qaug[b].b])
                        t1, t2, t3, t4, v = rope(qv[:, :, 64:96], pp.b, nh, 16, cosM, sinM,
                                                 None, j)
                        vv = lambda t: t[:, 0:nh * 16].rearrange("p (h d) -> p h d", d=16)
                        self.tt("pool", qo[:, :, 64:80], vv(t1), vv(t2), ALU.subtract, r=[t1.b, t2.b],
                                w=[Qaug[b].b])
                        self.tt("pool", qo[:, :, 80:96], vv(t3), vv(t4), ALU.add, r=[t3.b, t4.b], w=[Qaug[b].b])
                    else:
                        self.cp("dve", qo, qv, r=[pp.b], w=[Qaug[b].b])
                for h in range(8):
                    self.tr(pT1[0:96, h * 128:(h + 1) * 128], Qaug[b][:, h, :], ident[:],
                            r=[Qaug[b].b, ident.b], w=[pT1.b])
                self.cp("dve", QTs[b][0:96].rearrange("p k t -> p (k t)"), pT1[0:96, :], r=[pT1.b], w=[QTs[b].b])
                self.dma("dq1", sq.QT_mla[:, :, tok], QTs[b][0:96], r=[QTs[b].b], w=[sq.Qb])
                if CUT <= 5:
                    continue
                o4 = sqr[b][:, :].rearrange("p (hh g d) -> p g hh d", hh=4, g=2)
                if sq.rope:
                    xps = pF[4][:, :].rearrange("p (h d) -> p h d", d=64)
                    t1, t2, t3, t4, v = rope(xps, pF[4].b, 8, 32, cosS, sinS, None, j)
                    v4 = lambda t: t[:, 0:256].rearrange("p (g hh d) -> p g hh d", g=2, hh=4)
                    self.tt("pool", o4[:, :, :, 0:32], v4(t1), v4(t2), ALU.subtract, r=[t1.b, t2.b], w=[sqr[b].b])
                    self.tt("pool", o4[:, :, :, 32:64], v4(t3), v4(t4), ALU.add, r=[t3.b, t4.b], w=[sqr[b].b])
                else:
                    self.cp("dve", o4, pF[4][:, :].rearrange("p (g hh d) -> p g hh d", g=2, hh=4),
                            r=[pF[4].b], w=[sqr[b].b])
                for hh in range(4):
                    self.tr(pT2[:, 384 + hh * 128:384 + (hh + 1) * 128], sqr[b][:, hh * 128:(hh + 1) * 128],
                            ident[:], r=[sqr[b].b, ident.b], w=[pT2c])
                self.cp("dve", QsT[b][:].rearrange("p k t -> p (k t)"), pT2[:, 384:896], r=[pT2c], w=[QsT[b].b])
                self.dma("dq1", sq.QT_swa[:, :, tok], QsT[b][:], r=[QsT[b].b], w=[sq.Qb])
            self.S.flush()

    def phase_attn(self, l, sq):
        L = sq.L
        lat = sq.rope
        NKq = (CTX + L) if lat else CTX
        nkt = NKq // 128
        CQ = min(512, L)
        nsub = CQ // 128
        with ExitStack() as ph:
            T = lambda *a, **k: self.T(ph, *a, **k)
            KT = T("KT", [128, 4, NKq], BF16)
            V = T("V", [128, nkt, 4, 65], BF16)
            QTc = [T("QTc", [128, 4, CQ], BF16) for _ in range(2)]
            PT = [T("PT", [128, CQ], BF16) for _ in range(3)]
            yo = [T("yo", [128, nsub, 64], BF16) for _ in range(2)]
            rden = [T("rden", [128, 4], F32) for _ in range(2)]
            pS = [T("pS", [128, 512], F32, psum=True) for _ in range(2)]
            pO = [T("pO", [128, 512], F32, psum=True) for _ in range(4)]
            it = 0
            for hg in range(2):
                self.dma("dq0", KT[0:96, :, :], self.KT_mla[:, hg * 4:(hg + 1) * 4, 0:NKq], r=[self.b_kv], w=[KT.b])
                for t0 in range(0, nkt, 8):
                    t1_ = min(nkt, t0 + 8)
                    self.dma("dq0", V[:, t0:t1_], self.V_mla[t0 * 128:t1_ * 128, hg * 4:(hg + 1) * 4, :]
                             .rearrange("(t p) h d -> p t h d", p=128), r=[self.b_kv], w=[V.b])
                for qc in range(L // CQ):
                    Q = QTc[qc % 2]
                    self.dma("dq0", Q[0:96, :, :], sq.QT_mla[:, hg * 4:(hg + 1) * 4, qc * CQ:(qc + 1) * CQ],
                             r=[sq.Qb], w=[Q.b])
                    for h in range(4):
                        for kt in range(nkt):
                            ps = pS[it % 2]
                            pt = PT[it % 3]
                            it += 1
                            self.mm(ps[:, 0:CQ], KT[0:96, h, kt * 128:(kt + 1) * 128], Q[0:96, h, :], True, True,
                                    r=[KT.b, Q.b], w=[ps.b])
                            self.act(pt[:], ps[:, 0:CQ], AF.Exp, r=[ps.b], w=[pt.b], scale=MLA_SCALE)
                            for s_ in range(nsub):
                                self.mm(pO[s_][:, 0:65], pt[:, s_ * 128:(s_ + 1) * 128], V[:, kt, h, :],
                                        kt == 0, kt == nkt - 1, r=[pt.b, V.b], w=[pO[s_].b])
                        Y = yo[h % 2]
                        R_ = rden[h % 2]
                        for s_ in range(nsub):
                            self.S.op("dve", lambda e, s_=s_, R_=R_: e.reciprocal(out=R_[:, s_:s_ + 1],
                                                                                   in_=pO[s_][:, 64:65]),
                                      r=[pO[s_].b], w=[R_.b])
                            self.ts("dve", Y[:, s_, :], pO[s_][:, 0:64], R_[:, s_:s_ + 1], None, ALU.mult,
                                    r=[pO[s_].b, R_.b], w=[Y.b])
                        hd = hg * 4 + h
                        self.dma("dq1", sq.YA[qc * CQ:(qc + 1) * CQ, hd * 64:(hd + 1) * 64]
                                 .rearrange("(s p) d -> p s d", p=128), Y[:], r=[Y.b], w=[sq.Yb])
            self.S.flush()
        nt = L // 128
        cs = self.seqs["ctx"]
        with ExitStack() as ph:
            T = lambda *a, **k: self.T(ph, *a, **k)
            Kc = T("Kc", [128, CTX], BF16)
            Vc = T("Vc", [128, 2, 2, 65], BF16)
            es = T("es", [128, 8], F32)
            self.dma("dq0", Kc[:], cs.KT_swa[:, :], r=[self.b_kv], w=[Kc.b])
            self.memset("dve", Vc[:, :, :, 64:65], 1.0, w=[Vc.b])
            for t_ in range(2):
                self.dma("dq0", Vc[:, t_, :, 0:64], cs.V_swa[t_ * 128:(t_ + 1) * 128, :]
                         .rearrange("p (g d) -> p g d", g=2), r=[self.b_kv], w=[Vc.b])
            self.dma("dq0", es[:], self.swa_sink[l].partition_broadcast(128), w=[es.b])
            self.act(es[:], es[:], AF.Exp, r=[es.b], w=[es.b])
            if lat:
                Kl = T("Kl", [128, L], BF16)
                Vl = T("Vl", [128, nt, 2, 65], BF16)
                triL = T("triL", [128, 128], BF16)
                triR = T("triR", [128, 128], BF16)
                self.dma("dq0", Kl[:], sq.KT_swa[:, :], r=[self.b_kv], w=[Kl.b])
                self.memset("dve", Vl[:, :, :, 64:65], 1.0, w=[Vl.b])
                for t_ in range(nt):
                    self.dma("dq0", Vl[:, t_, :, 0:64], sq.V_swa[t_ * 128:(t_ + 1) * 128, :]
                             .rearrange("p (g d) -> p g d", g=2), r=[self.b_kv], w=[Vl.b])
                self.dma("dq0", triL[:], self.c_triL[:, :], w=[triL.b])
                self.dma("dq0", triR[:], self.c_triR[:, :], w=[triR.b])
            qt = [T("qt", [128, 4, 128], BF16) for _ in range(2)]
            PT = [T("PTs", [128, 512], BF16) for _ in range(3)]
            yb = [T("yb", [128, 8, 64], BF16) for _ in range(2)]
            dn = [T("dn", [128, 8], F32) for _ in range(2)]
            rd = [T("rd", [128, 8], F32) for _ in range(2)]
            pS = [T("pSs", [128, 512], F32, psum=True) for _ in range(2)]
            pO = [T("pOs", [128, 512], F32, psum=True) for _ in range(4)]
            it = 0
            for i in range(nt):
                Q = qt[i % 2]
                Y = yb[i % 2]
                self.dma("dq0", Q[:], sq.QT_swa[:, :, i * 128:(i + 1) * 128], r=[sq.Qb], w=[Q.b])
                for g in range(2):
                    pr = slice(64 * g, 64 * g + 64)
                    tiles = [("c", 0, None), ("c", 1, None)]
                    if lat:
                        if i > 0:
                            tiles.append(("l", i - 1, triL))
                        tiles.append(("l", i, None))
                        if i < nt - 1:
                            tiles.append(("l", i + 1, triR))
                    for ti, (kind, kt, msk) in enumerate(tiles):
                        ps = pS[it % 2]
                        pt = PT[it % 3]
                        it += 1
                        Ksrc, Vsrc = (Kc, Vc) if kind == "c" else (Kl, Vl)
                        self.mm(ps[:, :], Ksrc[pr, kt * 128:(kt + 1) * 128],
                                Q[pr, :, :].rearrange("p h t -> p (h t)"), True, True, r=[Ksrc.b, Q.b], w=[ps.b])
                        self.act(pt[:], ps[:, :], AF.Exp, r=[ps.b], w=[pt.b], scale=SWA_SCALE)
                        if msk is not None:
                            pv = pt[:, :].rearrange("p (h t) -> p h t", h=4)
                            self.tt("pool", pv, pv, bcast_mid(msk[:, :], 4), ALU.mult, r=[pt.b, msk.b], w=[pt.b])
                        for hh in range(4):
                            self.mm(pO[hh][:, 0:65], pt[:, hh * 128:(hh + 1) * 128], Vsrc[:, kt, g, :],
                                    ti == 0, ti == len(tiles) - 1, r=[pt.b, Vsrc.b], w=[pO[hh].b])
                    for hh in range(4):
                        hd = g * 4 + hh
                        D_, R_ = dn[i % 2], rd[i % 2]
                        self.tt("dve", D_[:, hd:hd + 1], pO[hh][:, 64:65], es[:, hd:hd + 1], ALU.add,
                                r=[pO[hh].b, es.b], w=[D_.b])
                        self.S.op("dve", lambda e, D_=D_, R_=R_, hd=hd: e.reciprocal(out=R_[:, hd:hd + 1],
                                                                                    in_=D_[:, hd:hd + 1]),
                                  r=[D_.b], w=[R_.b])
                        self.ts("dve", Y[:, hd, :], pO[hh][:, 0:64], R_[:, hd:hd + 1], None, ALU.mult,
                                r=[pO[hh].b, R_.b], w=[Y.b])
                self.dma("dq1", sq.YB[i * 128:(i + 1) * 128, :], Y[:].rearrange("p h d -> p (h d)"),
                         r=[Y.b], w=[sq.Yb])
            self.S.flush()

    def ln_rows(self, ph, l, sq, i0, i1):
        a = self.T(ph, "arow", [128, D], F32)
        b = self.T(ph, "brow", [128, D], F32)
        self.dma("dq0", a[:], self.modrow(l, sq.set, i0), r=[self.b_modt], w=[a.b])
        self.dma("dq0", b[:], self.modrow(l, sq.set, i1), r=[self.b_modt], w=[b.b])
        return a, b

    def phase_merge(self, l, sq):
        L = sq.L
        CQ = min(512, L)
        nsub = CQ // 128
        with ExitStack() as ph:
            T = lambda *a, **k: self.T(ph, *a, **k)
            Wg = T("Wg", [128, 8, 3072], BF16)
            Wbr = T("Wbr", [128, 12, 1024], BF16)
            Wout = T("Wout", [128, 8, 1024], BF16)
            ident = T("ident", [128, 128], BF16)
            g1row = T("g1row", [128, D], F32)
            self.dma("dq0", ident[:], self.c_ident[:, :], w=[ident.b])
            self.dma("dq0", g1row[:], self.modrow(l, sq.set, 2), r=[self.b_modt], w=[g1row.b])
            with ExitStack() as ph2:
                stages = [self.T(ph2, "wstage", [128, 8, 512], F32) for _ in range(2)]
                self.load_w(ph2, Wg, self.w_in[l][:, C_GT:C_GT + 3072], 8, 3072, stages)
                for br in range(3):
                    self.load_w(ph2, Wbr[:, br * 4:(br + 1) * 4, :], self.w_branch[l, br], 4, 1024, stages,
                                dstb=Wbr.b)
                self.load_w(ph2, Wout, self.w_out[l], 8, 1024, stages)
                self.S.flush()
            hT = [T("hTc", [128, 8, CQ], BF16) for _ in range(2)]
            yin = [T("yin", [128, 512], BF16) for _ in range(2)]
            yT = [[T("yT", [128, 4, CQ], BF16) for _ in range(2)] for _ in range(3)]
            sig = [T("sig", [128, CQ], F32) for _ in range(2)]
            tmpm = [T("tmpm", [128, CQ], F32) for _ in range(2)]
            acc = [T("acc", [128, CQ], F32) for _ in range(2)]
            mT = [T("mT", [128, 8, CQ], BF16) for _ in range(2)]
            xt = [T("xt", [128, D], F32) for _ in range(2)]
            yn = [T("yn", [128, D], F32) for _ in range(2)]
            junk = T("junk", [128, 512], BF16)
            s2 = [T("ss2", [128, 2], F32) for _ in range(2)]
            ss, s1, s2_, rs = ([T(n, [128, 1], F32) for _ in range(2)] for n in ("ss", "s1", "s2", "rs"))
            pT = T("pT", [128, 1024], BF16, psum=True)
            pG = [T("pG", [128, 512], F32, psum=True) for _ in range(2)]
            pB = [T("pB", [128, 512], F32, psum=True) for _ in range(2)]
            pY = [T("pY", [128, 512], F32, psum=True) for _ in range(2)]
            it = 0
            for c in range(L // CQ):
                cb = c % 2
                tokc = slice(c * CQ, (c + 1) * CQ)
                H = hT[cb]
                self.dma("dq0", H[:], sq.HT[:, :, tokc], r=[sq.HTb], w=[H.b])
                for bi, src in enumerate((sq.YA, sq.YB)):
                    Yt = yT[bi][cb]
                    for s_ in range(nsub):
                        yi = yin[(2 * c + s_ + bi) % 2]
                        self.dma("dq0", yi[:], src[c * CQ + s_ * 128:c * CQ + (s_ + 1) * 128, :], r=[sq.Yb], w=[yi.b])
                        for kc in range(4):
                            self.tr(pT[:, kc * 128:(kc + 1) * 128], yi[:, kc * 128:(kc + 1) * 128], ident[:],
                                    r=[yi.b, ident.b], w=[pT.b])
                        self.cp("dve", Yt[:, :, s_ * 128:(s_ + 1) * 128],
                                pT[:, 0:512].rearrange("p (k t) -> p k t", k=4), r=[pT.b], w=[Yt.b])
                Yc = yT[2][cb]
                self.dma("dq0", Yc[:], sq.YCT[:, tokc].rearrange("(k p) t -> p k t", p=128), r=[sq.Ycb], w=[Yc.b])
                M = mT[cb]
                for ft in range(8):
                    A = acc[ft % 2]
                    for br in range(3):
                        pg, pb = pG[it % 2], pB[it % 2]
                        sg, tm = sig[it % 2], tmpm[it % 2]
                        it += 1
                        gc = (br * 8 + ft) * 128
                        for k in range(8):
                            self.mm(pg[:, 0:CQ], Wg[:, k, gc:gc + 128], H[:, k, :], k == 0, k == 7,
                                    r=[Wg.b, H.b], w=[pg.b])
                        Yt = yT[br][cb]
                        for kc in range(4):
                            self.mm(pb[:, 0:CQ], Wbr[:, br * 4 + kc, ft * 128:(ft + 1) * 128], Yt[:, kc, :],
                                    kc == 0, kc == 3, r=[Wbr.b, Yt.b], w=[pb.b])
                        self.act(sg[:], pg[:, 0:CQ], AF.Sigmoid, r=[pg.b], w=[sg.b])
                        if br == 0:
                            self.tt("dve", A[:], pb[:, 0:CQ], sg[:], ALU.mult, r=[pb.b, sg.b], w=[A.b])
                        else:
                            self.tt("dve", tm[:], pb[:, 0:CQ], sg[:], ALU.mult, r=[pb.b, sg.b], w=[tm.b])
                            if br == 1:
                                self.tt("pool", A[:], A[:], tm[:], ALU.add, r=[A.b, tm.b], w=[A.b])
                            else:
                                self.tt("pool", M[:, ft, :], A[:], tm[:], ALU.add, r=[A.b, tm.b], w=[M.b])
                for s_ in range(nsub):
                    j = c * nsub + s_
                    b = j % 2
                    tok = slice(j * 128, (j + 1) * 128)
                    X, YN = xt[b], yn[b]
                    self.dma("dq0", X[:], sq.x[tok, :], r=[sq.xb], w=[X.b])
                    for hf in range(2):
                        for k in range(8):
                            self.mm(pY[hf][:, :], M[:, k, s_ * 128:(s_ + 1) * 128], Wout[:, k, hf * 512:(hf + 1) * 512],
                                    k == 0, k == 7, r=[M.b, Wout.b], w=[pY[hf].b])
                    for hf in range(2):
                        self.act(junk[:], pY[hf][:, :], AF.Square, r=[pY[hf].b], w=[junk.b, s2[b].b],
                                 accum=s2[b][:, hf:hf + 1])
                    self.tt("dve", ss[b][:], s2[b][:, 0:1], s2[b][:, 1:2], ALU.add, r=[s2[b].b], w=[ss[b].b])
                    self.rstd((s1[b], s2_[b]), ss[b], D, rs[b])
                    for hf in range(2):
                        hs = slice(hf * 512, (hf + 1) * 512)
                        self.stt("dve", YN[:, hs], pY[hf][:, :], rs[b][:], g1row[:, hs], ALU.mult, ALU.mult,
                                 r=[pY[hf].b, rs[b].b, g1row.b], w=[YN.b])
                    self.tt("pool", YN[:], YN[:], X[:], ALU.add, r=[YN.b, X.b], w=[YN.b])
                    self.dma("dq1", sq.XMID[tok, :], YN[:], r=[YN.b], w=[sq.Xmb])
            self.S.flush()

    def phase_ln2(self, l, sq):
        L = sq.L
        nt = L // 128
        with ExitStack() as ph:
            T = lambda *a, **k: self.T(ph, *a, **k)
            ident = T("ident", [128, 128], BF16)
            self.dma("dq0", ident[:], self.c_ident[:, :], w=[ident.b])
            a2row, sh2row = self.ln_rows(ph, l, sq, 3, 4)
            xt = [T("xt", [128, D], F32) for _ in range(2)]
            tmp = [T("tmp", [128, D], F32) for _ in range(2)]
            hb = [T("hb", [128, D], BF16) for _ in range(2)]
            hT = [T("hT", [128, 8, 128], BF16) for _ in range(2)]
            junk = T("junk", [128, D], BF16)
            ss, s1, s2, rs = ([T(n, [128, 1], F32) for _ in range(2)] for n in ("ss", "s1", "s2", "rs"))
            pT0 = T("pT0", [128, 1024], BF16, psum=True)
            for j in range(nt):
                b = j % 2
                tok = slice(j * 128, (j + 1) * 128)
                X, TMP, HB, HT = xt[b], tmp[b], hb[b], hT[b]
                self.dma("dq0", X[:], sq.XMID[tok, :], r=[sq.Xmb], w=[X.b])
                self.act(junk[:], X[:], AF.Square, r=[X.b], w=[junk.b, ss[b].b], accum=ss[b][:])
                self.rstd((s1[b], s2[b]), ss[b], D, rs[b])
                self.stt("dve", TMP[:], X[:], rs[b][:], a2row[:], ALU.mult, ALU.mult,
                         r=[X.b, rs[b].b, a2row.b], w=[TMP.b])
                self.tt("pool", HB[:], TMP[:], sh2row[:], ALU.add, r=[TMP.b, sh2row.b], w=[HB.b])
                for k in range(8):
                    self.tr(pT0[:, k * 128:(k + 1) * 128], HB[:, k * 128:(k + 1) * 128], ident[:],
                            r=[HB.b, ident.b], w=[pT0.b])
                self.cp("act", HT[:].rearrange("p k t -> p (k t)"), pT0[:], r=[pT0.b], w=[HT.b])
                self.dma("dq1", sq.H2T[:, :, tok], HT[:], r=[HT.b], w=[sq.H2b])
            self.S.flush()

    def phase_ffn(self, l, sq, xout):
        L = sq.L
        C = 256
        NF = D_FF // 128
        with ExitStack() as ph:
            T = lambda *a, **k: self.T(ph, *a, **k)
            Wup = T("Wup", [128, 8, 2 * D_FF], BF16)
            Wdn = T("Wdn", [128, NF, D], BF16)
            cw = T("cw", [128, 44, 3], F32)
            cbias = T("cbias", [128, 44], F32)
            g2row = T("g2row", [128, D], F32)
            self.dma("dq0", cw[:], self.ffn_conv_w[l], w=[cw.b])
            self.dma("dq0", cbias[:], self.ffn_conv_b[l], w=[cbias.b])
            self.dma("dq0", g2row[:], self.modrow(l, sq.set, 5), r=[self.b_modt], w=[g2row.b])
            with ExitStack() as ph2:
                stages = [self.T(ph2, "wstage", [128, 22, 256], F32) for _ in range(2)]
                self.load_w(ph2, Wup, self.w_up[l], 8, 2 * D_FF, stages, cw=256)
                self.load_w(ph2, Wdn, self.w_down[l], NF, D, stages, cw=256)
                self.S.flush()
            h2 = [T("h2c", [128, 8, C + 2], BF16) for _ in range(2)]
            ua = [T("ua", [128, C], F32) for _ in range(2)]
            ub = [T("ub", [128, C], F32) for _ in range(2)]
            sa = [T("sa", [128, C], F32) for _ in range(2)]
            aT = [T("aT", [128, NF, C], BF16) for _ in range(2)]
            xt = [T("xt", [128, D], F32) for _ in range(2)]
            yn = [T("yn", [128, D], F32) for _ in range(2)]
            junk = T("junk", [128, 512], BF16)
            s2 = [T("ss2", [128, 2], F32) for _ in range(2)]
            ss, s1, s2_, rs = ([T(n, [128, 1], F32) for _ in range(2)] for n in ("ss", "s1", "s2", "rs"))
            pA = [T("pA", [128, 512], F32, psum=True) for _ in range(2)]
            pBb = [T("pBb", [128, 512], F32, psum=True) for _ in range(2)]
            pY = [T("pY", [128, 512], F32, psum=True) for _ in range(2)]
            it = 0
            for c in range(L // C):
                cb = c % 2
                c0 = c * C
                H = h2[cb]
                lo, hi = max(c0 - 1, 0), min(c0 + C + 1, L)
                if c0 == 0:
                    self.memset("dve", H[:, :, 0:1], 0.0, w=[H.b])
                if c0 + C == L:
                    self.memset("dve", H[:, :, C + 1:C + 2], 0.0, w=[H.b])
                self.dma("dq0", H[:, :, lo - (c0 - 1):hi - (c0 - 1)], sq.H2T[:, :, lo:hi], r=[sq.H2b], w=[H.b])
                A_T = aT[cb]
                for f in range(NF):
                    pa, pb = pA[it % 2], pBb[it % 2]
                    UA, UB, SA = ua[it % 2], ub[it % 2], sa[it % 2]
                    it += 1
                    for (pp, col0) in ((pa, f * 128), (pb, D_FF + f * 128)):
                        for k in range(8):
                            self.mm(pp[:, 0:C + 2], Wup[:, k, col0:col0 + 128], H[:, k, :], k == 0, k == 7,
                                    r=[Wup.b, H.b], w=[pp.b])
                    for (pp, U, fi) in ((pa, UA, f), (pb, UB, NF + f)):
                        self.act(U[:], pp[:, 1:C + 1], AF.Identity, r=[pp.b, cw.b, cbias.b], w=[U.b],
                                 scale=cw[:, fi, 1:2], bias=cbias[:, fi:fi + 1])
                        self.stt("dve", U[:], pp[:, 0:C], cw[:, fi, 0:1], U[:], ALU.mult, ALU.add,
                                 r=[pp.b, cw.b, U.b], w=[U.b])
                        self.stt("dve", U[:], pp[:, 2:C + 2], cw[:, fi, 2:3], U[:], ALU.mult, ALU.add,
                                 r=[pp.b, cw.b, U.b], w=[U.b])
                    self.act(SA[:], UA[:], AF.Silu, r=[UA.b], w=[SA.b])
                    self.tt("pool", A_T[:, f, :], SA[:], UB[:], ALU.mult, r=[SA.b, UB.b], w=[A_T.b])
                for s_ in range(C // 128):
                    j = c * (C // 128) + s_
                    b = j % 2
                    tok = slice(j * 128, (j + 1) * 128)
                    X, YN = xt[b], yn[b]
                    self.dma("dq0", X[:], sq.XMID[tok, :], r=[sq.Xmb], w=[X.b])
                    for hf in range(2):
                        for k in range(NF):
                            self.mm(pY[hf][:, :], A_T[:, k, s_ * 128:(s_ + 1) * 128], Wdn[:, k, hf * 512:(hf + 1) * 512],
                                    k == 0, k == NF - 1, r=[A_T.b, Wdn.b], w=[pY[hf].b])
                    for hf in range(2):
                        self.act(junk[:], pY[hf][:, :], AF.Square, r=[pY[hf].b], w=[junk.b, s2[b].b],
                                 accum=s2[b][:, hf:hf + 1])
                    self.tt("dve", ss[b][:], s2[b][:, 0:1], s2[b][:, 1:2], ALU.add, r=[s2[b].b], w=[ss[b].b])
                    self.rstd((s1[b], s2_[b]), ss[b], D, rs[b])
                    for hf in range(2):
                        hs = slice(hf * 512, (hf + 1) * 512)
                        self.stt("dve", YN[:, hs], pY[hf][:, :], rs[b][:], g2row[:, hs], ALU.mult, ALU.mult,
                                 r=[pY[hf].b, rs[b].b, g2row.b], w=[YN.b])
                    self.tt("pool", YN[:], YN[:], X[:], ALU.add, r=[YN.b, X.b], w=[YN.b])
                    self.dma("dq1", xout[tok, :], YN[:], r=[YN.b], w=[sq.xob])
            self.S.flush()

    def hy_tabs(self, L):
        if L not in self._hyc:
            t = fft_tables(L)
            h = hyena_consts(L)
            d = {}
            for k, v in list(t.items()) + list(h.items()):
                d[k] = self.const("c_%s_%d" % (k, L), v)
            self._hyc[L] = d
        return self._hyc[L]

    def fft_fwd(self, L, src, srcb, cb):
        H1 = L // 128
        KH = H1 + 1
        tb = self.hy_tabs(L)
        with ExitStack() as ph:
            T = lambda *a, **k: self.T(ph, *a, **k)
            xin = T("xin", [H1, 128, 128], BF16)
            F1 = T("F1", [H1, 2 * KH], BF16)
            A = T("A", [128, KH, 3, 128], BF16)
            G = T("G", [128, KH, 2, 128], BF16)
            pA = [T("pA", [128, 512], F32, psum=True) for _ in range(2)]
            pX = [T("pX", [128, 512], F32, psum=True) for _ in range(2)]
            for c8 in range(0, 128, 16):
                self.dma("dq0", xin[:, c8:c8 + 16, :], src[c8:c8 + 16, :].rearrange("c (a b) -> a c b", b=128),
                         r=[srcb], w=[xin.b])
            self.dma("dq0", F1[:], tb["F1"][:, :], w=[F1.b])
            self.dma("dq0", G[:], tb["G"][:, :, :, :], w=[G.b])
            nbm = max(1, 512 // (2 * KH))
            bi = 0
            for c0 in range(0, 128, nbm):
                nb = min(nbm, 128 - c0)
                p = pA[bi % 2]
                bi += 1
                for j in range(nb):
                    self.mm(p[:, j * 2 * KH:(j + 1) * 2 * KH], xin[0:H1, c0 + j, :], F1[0:H1, :], True, True,
                            r=[xin.b, F1.b], w=[p.b])
                pv = p[:, 0:nb * 2 * KH].rearrange("p (c k) -> p k c", k=2 * KH)
                self.cp("dve", A[:, :, 1, c0:c0 + nb], pv[:, 0:KH, :], r=[p.b], w=[A.b])
                self.cp("dve", A[:, :, 2, c0:c0 + nb], pv[:, KH:2 * KH, :], r=[p.b], w=[A.b])
                self.ts("dve", A[:, :, 0, c0:c0 + nb], pv[:, KH:2 * KH, :], -1.0, None, ALU.mult, r=[p.b], w=[A.b])
            bi = 0
            for k0 in range(0, KH, 2):
                nk = min(2, KH - k0)
                p = pX[bi % 2]
                bi += 1
                for kk in range(nk):
                    k1 = k0 + kk
                    o_ = p[:, kk * 256:(kk + 1) * 256]
                    self.mm(o_, G[:, k1, 0, :], A[:, k1, 1:3, :].rearrange("p a c -> p (a c)"), True, False,
                            r=[G.b, A.b], w=[p.b])
                    self.mm(o_, G[:, k1, 1, :], A[:, k1, 0:2, :].rearrange("p a c -> p (a c)"), False, True,
                            r=[G.b, A.b], w=[p.b])
                cb(k0, nk, p)
            self.S.flush()

    def fft_inv(self, L, Y, dst, dstb):
        H1 = L // 128
        KH = H1 + 1
        NCH = 64
        NB = 512 // NCH
        tb = self.hy_tabs(L)
        with ExitStack() as ph:
            T = lambda *a, **k: self.T(ph, *a, **k)
            Finv = T("Finv", [128, 3, 128], BF16)
            Pt = T("Pt", [KH, 128, 2, H1], BF16)
            nbuf = 2 if L <= 1024 else 1
            Ds = [T("Ds", [KH, 128, 2, NCH], BF16) for _ in range(nbuf)]
            yt = [T("yt", [H1, NCH, 128], F32) for _ in range(nbuf)]
            pC = [T("pC", [128, 512], F32, psum=True) for _ in range(2)]
            pY = [T("pYy", [128, 512], F32, psum=True) for _ in range(2)]
            self.dma("dq0", Finv[:], tb["Finv"][:, :, :], w=[Finv.b])
            self.dma("dq0", Pt[:], tb["P"][:, :, :, :], w=[Pt.b])
            bi = 0
            for sub in range(128 // NCH):
                D_ = Ds[sub % nbuf]
                YT = yt[sub % nbuf]
                for cl in range(0, NCH, 2):
                    p = pC[bi % 2]
                    bi += 1
                    for j in range(2):
                        ch = sub * NCH + cl + j
                        o_ = p[0:KH, j * 256:(j + 1) * 256]
                        self.mm(o_, Y[:, 0, ch, :], Finv[:, 1:3, :].rearrange("p a n -> p (a n)"), True, False,
                                r=[Y.b, Finv.b], w=[p.b])
                        self.mm(o_, Y[:, 1, ch, :], Finv[:, 0:2, :].rearrange("p a n -> p (a n)"), False, True,
                                r=[Y.b, Finv.b], w=[p.b])
                    self.cp("dve", D_[0:KH, :, :, cl:cl + 2],
                            p[0:KH, :].rearrange("p (c a n) -> p n a c", c=2, a=2), r=[p.b], w=[D_.b])
                for n0 in range(0, 128, NB):
                    p = pY[bi % 2]
                    bi += 1
                    for j in range(NB):
                        n2 = n0 + j
                        o_ = p[0:H1, j * NCH:(j + 1) * NCH]
                        self.mm(o_, Pt[0:KH, n2, 0, :], D_[0:KH, n2, 0, :], True, False, r=[Pt.b, D_.b], w=[p.b])
                        self.mm(o_, Pt[0:KH, n2, 1, :], D_[0:KH, n2, 1, :], False, True, r=[Pt.b, D_.b], w=[p.b])
                    self.cp("dve", YT[0:H1, :, n0:n0 + NB], p[0:H1, :].rearrange("p (n c) -> p c n", c=NCH),
                            r=[p.b], w=[YT.b])
                for c8 in range(0, NCH, 16):
                    self.dma("dq1", dst[sub * NCH + c8:sub * NCH + c8 + 16, :].rearrange("c (a b) -> a c b", b=128),
                             YT[0:H1, c8:c8 + 16, :], r=[YT.b], w=[dstb])
            self.S.flush()

    def phase_hy_proj(self, l, sq):
        L = sq.L
        C = 256
        with ExitStack() as ph:
            T = lambda *a, **k: self.T(ph, *a, **k)
            Why = T("Why", [128, 8, 1536], BF16)
            cw = T("cw", [128, 12, 3], F32)
            cbias = T("cbias", [128, 12], F32)
            self.dma("dq0", cw[:], self.hy_conv_w[l], w=[cw.b])
            self.dma("dq0", cbias[:], self.hy_conv_b[l], w=[cbias.b])
            with ExitStack() as ph2:
                stages = [self.T(ph2, "wstage", [128, 8, 512], F32) for _ in range(2)]
                self.load_w(ph2, Why, self.w_in[l][:, C_HY:C_HY + 1536], 8, 1536, stages)
                self.S.flush()
            Hc = [T("Hc", [128, 8, C + 2], BF16) for _ in range(2)]
            U = [T("U", [128, 12, C], F32) for _ in range(2)]
            Ub = [T("Ub", [128, 4, C], BF16) for _ in range(2)]
            pp_ = [T("pU", [128, 512], F32, psum=True) for _ in range(3)]
            it = 0
            for c in range(L // C):
                cb_ = c % 2
                c0 = c * C
                H = Hc[cb_]
                lo, hi = max(c0 - 1, 0), min(c0 + C + 1, L)
                if c0 == 0:
                    self.memset("dve", H[:, :, 0:1], 0.0, w=[H.b])
                if c0 + C == L:
                    self.memset("dve", H[:, :, C + 1:C + 2], 0.0, w=[H.b])
                self.dma("dq0", H[:, :, lo - (c0 - 1):hi - (c0 - 1)], sq.HT[:, :, lo:hi], r=[sq.HTb], w=[H.b])
                UU = U[cb_]
                for i in range(12):
                    pp = pp_[it % 3]
                    it += 1
                    for k in range(8):
                        self.mm(pp[:, 0:C + 2], Why[:, k, i * 128:(i + 1) * 128], H[:, k, :], k == 0, k == 7,
                                r=[Why.b, H.b], w=[pp.b])
                    self.act(UU[:, i, :], pp[:, 1:C + 1], AF.Identity, r=[pp.b, cw.b, cbias.b], w=[UU.b],
                             scale=cw[:, i, 1:2], bias=cbias[:, i:i + 1])
                    self.stt("dve", UU[:, i, :], pp[:, 0:C], cw[:, i, 0:1], UU[:, i, :], ALU.mult, ALU.add,
                             r=[pp.b, cw.b, UU.b], w=[UU.b])
                    self.stt("dve", UU[:, i, :], pp[:, 2:C + 2], cw[:, i, 2:3], UU[:, i, :], ALU.mult, ALU.add,
                             r=[pp.b, cw.b, UU.b], w=[UU.b])
                self.cp("pool", Ub[cb_][:], UU[:, 0:4, :], r=[UU.b], w=[Ub[cb_].b])
                self.dma("dq1", sq.UC[:, c0:c0 + C].rearrange("(i p) t -> p i t", p=128), UU[:], r=[UU.b], w=[sq.UCb])
                self.dma("dq1", sq.ZB[:, c0:c0 + C].rearrange("(i p) t -> p i t", p=128), Ub[cb_][:],
                         r=[Ub[cb_].b], w=[sq.ZBb])
            self.S.flush()

    def phase_hy_filters(self, l, sq):
        L = sq.L
        H1 = L // 128
        KH = H1 + 1
        CH = min(512, L)
        nch = L // CH
        tb = self.hy_tabs(L)
        PI = math.pi
        with ExitStack() as ph:
            T = lambda *a, **k: self.T(ph, *a, **k)
            hid2 = T("hid2", [64, L], F32)
            w3 = T("w3", [64, 2048], F32)
            self.dma("dq0", w3[:], self.hy_w3[l], w=[w3.b])
            with ExitStack() as ph1:
                T1 = lambda *a, **k: self.T(ph1, *a, **k)
                feats = T1("feats", [HY_EMB, L], F32)
                hid1 = T1("hid1", [64, L], F32)
                w1 = T1("w1", [HY_EMB, 64], F32)
                w2 = T1("w2", [64, 64], F32)
                fr = T1("fr", [64, 2], F32)
                bb = T1("bb", [64, 2], F32)
                fb = T1("fb", [64, 2], F32)
                arg = [T1("arg", [64, CH], F32) for _ in range(2)]
                m1 = [T1("m1", [64, CH], F32) for _ in range(2)]
                pm = [T1("pm", [128, 512], F32, psum=True) for _ in range(2)]
                self.dma("dq0", feats[:], tb["featsT"][:, :], w=[feats.b])
                self.dma("dq0", w1[:], self.hy_w1[l], w=[w1.b])
                self.dma("dq0", w2[:], self.hy_w2[l], w=[w2.b])
                self.dma("dq0", fr[:], self.hy_freq[l], w=[fr.b])
                self.dma("dq0", bb[:, 0:1], self.hy_b1[l], w=[bb.b])
                self.dma("dq0", bb[:, 1:2], self.hy_b2[l], w=[bb.b])
                self.tt("dve", fb[:], fr[:], bb[:], ALU.mult, r=[fr.b, bb.b], w=[fb.b])
                it = 0
                for li, (wt, KK, src, dst) in enumerate(((w1, HY_EMB, feats, hid1), (w2, 64, hid1, hid2))):
                    for c in range(nch):
                        cs = slice(c * CH, (c + 1) * CH)
                        p, a, m = pm[it % 2], arg[it % 2], m1[it % 2]
                        it += 1
                        self.mm(p[0:64, 0:CH], wt[0:KK, :], src[0:KK, cs], True, True, r=[wt.b, src.b], w=[p.b])
                        self.act(a[:], p[0:64, 0:CH], AF.Identity, r=[p.b, fr.b, fb.b], w=[a.b],
                                 scale=fr[:, li:li + 1], bias=fb[:, li:li + 1])
                        self.ts("dve", m[:], a[:], PI, -2 * PI, ALU.is_gt, ALU.mult, r=[a.b], w=[m.b])
                        self.tt("dve", a[:], a[:], m[:], ALU.add, r=[a.b, m.b], w=[a.b])
                        self.ts("dve", m[:], a[:], -PI, 2 * PI, ALU.is_lt, ALU.mult, r=[a.b], w=[m.b])
                        self.tt("dve", a[:], a[:], m[:], ALU.add, r=[a.b, m.b], w=[a.b])
                        self.act(dst[:, cs], a[:], AF.Sin, r=[a.b], w=[dst.b])
                self.S.flush()
            with ExitStack() as ph2:
                T2 = lambda *a, **k: self.T(ph2, *a, **k)
                hf = T2("hf", [128, L], F32)
                hb = T2("hb", [128, L], F32)
                so = [T2("so", [128, L], BF16) for _ in range(2)]
                dsc = T2("dsc", [128, 4], F32)
                dbias = T2("dbias", [128, 4, nch], F32)
                iota = T2("iota", [128, CH], F32)
                dec = [T2("dec", [128, CH], F32) for _ in range(2)]
                sm = [T2("sm", [128, 4], F32) for _ in range(2)]
                pw = [T2("pw", [128, 512], F32, psum=True) for _ in range(2)]
                self.dma("dq0", dsc[:], tb["dsc"][:, :], w=[dsc.b])
                self.dma("dq0", dbias[:], tb["dbias"][:, :, :], w=[dbias.b])
                self.dma("dq0", iota[:], tb["iota"][:, :], w=[iota.b])
                it = 0
                for g in range(4):
                    for o in range(2):
                        SM = sm[(g * 2 + o) % 2]
                        for di, hh_ in enumerate((hf, hb)):
                            col = di * 1024 + o * 512 + g * 128
                            for c in range(nch):
                                cs = slice(c * CH, (c + 1) * CH)
                                p, dc = pw[it % 2], dec[it % 2]
                                it += 1
                                self.mm(p[:, 0:CH], w3[0:64, col:col + 128], hid2[0:64, cs], True, True,
                                        r=[w3.b, hid2.b], w=[p.b])
                                self.act(dc[:], iota[:], AF.Exp, r=[iota.b, dsc.b, dbias.b], w=[dc.b],
                                         scale=dsc[:, g:g + 1], bias=dbias[:, g, c:c + 1])
                                self.tt("dve", hh_[:, cs], p[:, 0:CH], dc[:], ALU.mult, r=[p.b, dc.b], w=[hh_.b])
                        self.memset("dve", hb[:, 0:1], 0.0, w=[hb.b])
                        for di, hh_ in enumerate((hf, hb)):
                            self.S.op("dve", lambda e, hh_=hh_, di=di, SM=SM: e.tensor_reduce(
                                out=SM[:, di:di + 1], in_=hh_[:], axis=AX.X, op=ALU.add, apply_absolute_value=True),
                                r=[hh_.b], w=[SM.b])
                        self.tt("dve", SM[:, 2:3], SM[:, 0:1], SM[:, 1:2], ALU.add, r=[SM.b], w=[SM.b])
                        self.S.op("dve", lambda e, SM=SM: e.reciprocal(out=SM[:, 3:4], in_=SM[:, 2:3]),
                                  r=[SM.b], w=[SM.b])
                        for si, op_ in enumerate((ALU.add, ALU.subtract)):
                            O = so[si]
                            self.stt("dve" if si == 0 else "pool", O[:], hf[:], 0.0, hb[:], ALU.add, op_,
                                     r=[hf.b, hb.b], w=[O.b]) if si == 0 else \
                                self.tt("pool", O[:], hf[:], hb[:], op_, r=[hf.b, hb.b], w=[O.b])
                            self.ts("dve", O[:], O[:], SM[:, 3:4], None, ALU.mult, r=[O.b, SM.b], w=[O.b])
                            rows = slice(o * 512 + g * 128, o * 512 + (g + 1) * 128)
                            self.dma("dq1", sq.FT[si, rows, :], O[:], r=[O.b], w=[sq.FTb])
                self.S.flush()
        for o in range(2):
            for g in range(4):
                rows = slice(o * 512 + g * 128, o * 512 + (g + 1) * 128)
                with ExitStack() as ph:
                    Hs = self.T(ph, "Hs", [128, KH, 3, 128], BF16)

                    def cb_s(k0, nk, p, Hs=Hs):
                        pv = p[:, 0:nk * 256].rearrange("p (k a c) -> p k a c", a=2, c=128)
                        self.cp("dve", Hs[:, k0:k0 + nk, 1, :], pv[:, :, 0, :], r=[p.b], w=[Hs.b])

                    def cb_d(k0, nk, p, Hs=Hs):
                        pv = p[:, 0:nk * 256].rearrange("p (k a c) -> p k a c", a=2, c=128)
                        self.cp("dve", Hs[:, k0:k0 + nk, 2, :], pv[:, :, 1, :], r=[p.b], w=[Hs.b])
                        self.ts("dve", Hs[:, k0:k0 + nk, 0, :], pv[:, :, 1, :], -1.0, None, ALU.mult, r=[p.b],
                                w=[Hs.b])
                    self.fft_fwd(L, sq.FT[0, rows, :], sq.FTb, cb_s)
                    self.fft_fwd(L, sq.FT[1, rows, :], sq.FTb, cb_d)
                    self.dma("dq1", sq.HS[o, g], Hs[:], r=[Hs.b], w=[sq.HSb])
                    self.S.flush()

    def phase_hy_conv(self, l, sq):
        L = sq.L
        H1 = L // 128
        KH = H1 + 1
        CHK = min(2048, L)
        for o in range(2):
            for g in range(4):
                rows = slice(g * 128, (g + 1) * 128)
                with ExitStack() as ph:
                    T = lambda *a, **k: self.T(ph, *a, **k)
                    Y = T("Y", [128, 2, 128, KH], BF16)
                    with ExitStack() as phf:
                        Hb = [self.T(phf, "Hb", [128, 2, 3, 128], BF16) for _ in range(2)]
                        T1 = [self.T(phf, "T1", [128, 2, 2, 128], F32) for _ in range(2)]
                        T2 = [self.T(phf, "T2", [128, 2, 2, 128], F32) for _ in range(2)]
                        cnt = [0]

                        def cb(k0, nk, p, Y=Y, Hb=Hb, T1=T1, T2=T2, cnt=cnt, o=o, g=g):
                            i = cnt[0] % 2
                            cnt[0] += 1
                            hb_, t1, t2 = Hb[i], T1[i], T2[i]
                            self.dma("dq0", hb_[:, 0:nk], sq.HS[o, g][:, k0:k0 + nk], r=[sq.HSb], w=[hb_.b])
                            base = p[:, 0:nk * 256]
                            xr = bass.AP(base.tensor, base.offset, [list(base.ap[0]), [256, nk], [0, 2], [1, 128]])
                            xi = bass.AP(base.tensor, base.offset + 128,
                                         [list(base.ap[0]), [256, nk], [0, 2], [1, 128]])
                            self.tt("dve", t1[:, 0:nk], xr, hb_[:, 0:nk, 1:3, :], ALU.mult, r=[p.b, hb_.b], w=[t1.b])
                            self.tt("dve", t2[:, 0:nk], xi, hb_[:, 0:nk, 0:2, :], ALU.mult, r=[p.b, hb_.b], w=[t2.b])
                            self.tt("dve", Y[:, :, :, k0:k0 + nk].rearrange("p a c k -> p k a c"), t1[:, 0:nk],
                                    t2[:, 0:nk], ALU.add, r=[t1.b, t2.b], w=[Y.b])
                        self.fft_fwd(L, sq.ZB[rows, :], sq.ZBb, cb)
                    self.fft_inv(L, Y, sq.CV[rows, :], sq.CVb)
                with ExitStack() as ph:
                    T = lambda *a, **k: self.T(ph, *a, **k)
                    sk = T("sk", [128, 2, 4], F32)
                    self.dma("dq0", sk[:], self.hy_skip[l], w=[sk.b])
                    cv = [T("cv", [128, CHK], F32) for _ in range(2)]
                    zz = [T("zz", [128, CHK], F32) for _ in range(2)]
                    gt = [T("gt", [128, CHK], F32) for _ in range(2)]
                    zb = [T("zb", [128, CHK], BF16) for _ in range(2)]
                    zsrc = sq.UC if o == 0 else sq.Z2
                    zsb = sq.UCb if o == 0 else sq.Z2b
                    grow = slice(512 * (o + 1) + g * 128, 512 * (o + 1) + (g + 1) * 128)
                    for c in range(L // CHK):
                        i = c % 2
                        cs = slice(c * CHK, (c + 1) * CHK)
                        self.dma("dq0", cv[i][:], sq.CV[rows, cs], r=[sq.CVb], w=[cv[i].b])
                        self.dma("dq0", zz[i][:], zsrc[rows, cs], r=[zsb], w=[zz[i].b])
                        self.dma("dq0", gt[i][:], sq.UC[grow, cs], r=[sq.UCb], w=[gt[i].b])
                        self.stt("dve", zz[i][:], zz[i][:], sk[:, o, g:g + 1], cv[i][:], ALU.mult, ALU.add,
                                 r=[zz[i].b, sk.b, cv[i].b], w=[zz[i].b])
                        self.tt("pool", zz[i][:], zz[i][:], gt[i][:], ALU.mult, r=[zz[i].b, gt[i].b], w=[zz[i].b])
                        self.cp("dve", zb[i][:], zz[i][:], r=[zz[i].b], w=[zb[i].b])
                        if o == 0:
                            self.dma("dq1", sq.Z2[rows, cs], zz[i][:], r=[zz[i].b], w=[sq.Z2b])
                            self.dma("dq1", sq.ZB2[rows, cs], zb[i][:], r=[zb[i].b], w=[sq.ZBb])
                        else:
                            self.dma("dq1", sq.YCT[rows, cs], zb[i][:], r=[zb[i].b], w=[sq.Ycb])
                    self.S.flush()
            if o == 0:
                sq.ZB, sq.ZB2 = sq.ZB2, sq.ZB

    def phase_hyena(self, l, sq):
        self.phase_hy_proj(l, sq)
        self.phase_hy_filters(l, sq)
        self.phase_hy_conv(l, sq)

    def declare(self):
        L = self.L
        NK = self.NK
        dp = self.depth
        I = self.inp
        self.x = I("x", [L, D])
        self.ctx = I("ctx", [CTX, D])
        self.cT = I("cT", [128, 8, 2])
        self.w_mod = I("w_mod", [dp, D, 6 * D])
        self.b_mod = I("b_mod", [dp, 6 * D])
        self.norm_g = I("norm_g", [dp, 4, D])
        self.w_in = I("w_in", [dp, D, IN_W])
        self.kv_norm_g = I("kv_norm_g", [dp, 128, 2])
        self.w_kv_up = I("w_kv_up", [dp, KV_RANK, 1024])
        self.swa_sink = I("swa_sink", [dp, 8])
        self.hy_conv_w = I("hy_conv_w", [dp, 128, 12, 3])
        self.hy_conv_b = I("hy_conv_b", [dp, 128, 12])
        self.hy_w1 = I("hy_w1", [dp, HY_EMB, HY_HID])
        self.hy_b1 = I("hy_b1", [dp, HY_HID, 1])
        self.hy_w2 = I("hy_w2", [dp, HY_HID, HY_HID])
        self.hy_b2 = I("hy_b2", [dp, HY_HID, 1])
        self.hy_freq = I("hy_freq", [dp, HY_HID, 2])
        self.hy_w3 = I("hy_w3", [dp, HY_HID, 4 * HY_W])
        self.hy_skip = I("hy_skip", [dp, 128, 2, 4])
        self.w_branch = I("w_branch", [dp, 3, 512, D])
        self.w_out = I("w_out", [dp, D, D])
        self.w_up = I("w_up", [dp, D, 2 * D_FF])
        self.ffn_conv_w = I("ffn_conv_w", [dp, 128, 44, 3])
        self.ffn_conv_b = I("ffn_conv_b", [dp, 128, 44])
        self.w_down = I("w_down", [dp, D_FF, D])
        C = self.const
        self.c_ident = C("c_ident", _bf(np.eye(128)))
        self.c_identf = C("c_identf", _f32(np.eye(128)))
        pos = np.arange(L)
        cm, sm = rope_tables(pos, MLA_ROPE)
        cs, sn = rope_tables(pos, SWA_D)
        self.c_cosM = C("c_cosM", tile_major(cm))
        self.c_sinM = C("c_sinM", tile_major(sm))
        self.c_cosS = C("c_cosS", tile_major(cs))
        self.c_sinS = C("c_sinS", tile_major(sn))
        kk = np.arange(128)[:, None]
        qq = np.arange(128)[None, :]
        self.c_triL = C("c_triL", _bf((kk >= qq).astype(np.float32)))
        self.c_triR = C("c_triR", _bf((kk <= qq).astype(np.float32)))
        Sc = self.scr
        self.MODT = Sc("MODT", [dp, 2, 6, D])
        self.b_modt = Buf("modt")
        self.KT_mla = Sc("KT_mla", [96, 8, NK], BF16)
        self.V_mla = Sc("V_mla", [NK, 8, 65], BF16)
        self.b_kv = Buf("kv")
        self.seqs = {}
        for name, Ls, st, rope, key0 in (("ctx", CTX, 1, False, 0), ("lat", L, 0, True, CTX)):
            s = Seq()
            s.name, s.L, s.set, s.rope, s.key0 = name, Ls, st, rope, key0
            s.HT = Sc("HT_" + name, [128, 8, Ls], BF16)
            s.HTb = Buf()
            s.QT_mla = Sc("QT_mla_" + name, [96, 8, Ls], BF16)
            s.QT_swa = Sc("QT_swa_" + name, [128, 4, Ls], BF16)
            s.Qb = Buf()
            s.KT_swa = Sc("KT_swa_" + name, [128, Ls], BF16)
            s.V_swa = Sc("V_swa_" + name, [Ls, 128], BF16)
            s.xb = Buf()
            s.YA = Sc("YA_" + name, [Ls, 512], BF16)
            s.YB = Sc("YB_" + name, [Ls, 512], BF16)
            s.Yb = Buf()
            if "YCT_in" in self.dbg:
                s.YCT = self.inp("YCT_" + name, [512, Ls], BF16)
            else:
                s.YCT = Sc("YCT_" + name, [512, Ls], BF16)
            s.Ycb = Buf()
            s.XMID = Sc("XMID_" + name, [Ls, D])
            s.Xmb = Buf()
            s.UC = Sc("UC_" + name, [1536, Ls])
            s.UCb = Buf()
            s.ZB = Sc("ZB_" + name, [512, Ls], BF16)
            s.ZB2 = Sc("ZB2_" + name, [512, Ls], BF16)
            s.ZBb = Buf()
            s.Z2 = Sc("Z2_" + name, [512, Ls])
            s.Z2b = Buf()
            s.CV = Sc("CV_" + name, [512, Ls])
            s.CVb = Buf()
            s.FT = Sc("FT_" + name, [2, 1024, Ls], BF16)
            s.FTb = Buf()
            s.HS = Sc("HS_" + name, [2, 4, 128, Ls // 128 + 1, 3, 128], BF16)
            s.HSb = Buf()
            s.H2T = Sc("H2T_" + name, [128, 8, Ls], BF16)
            s.H2b = Buf()
            s.XO = Sc("XO_" + name, [Ls, D])
            s.xob = Buf()
            self.seqs[name] = s
        self.X1 = self.seqs["lat"].XO
        self.XC1 = self.seqs["ctx"].XO
        self.y = self.nc.dram_tensor("y", [self.NQ, D], F32, kind="ExternalOutput").ap()
        self.outs["y"] = self.y

    def build(self):
        nc = self.nc
        self.declare()
        with ExitStack() as es:
            self.S = Sched(nc, es)
            for l in range(self.depth):
                if self.layer(l):
                    break
        return nc

    def layer(self, l):
        lat, ctx = self.seqs["lat"], self.seqs["ctx"]
        lat.x = self.x if l == 0 else self.X1
        ctx.x = self.ctx if l == 0 else self.XC1
        self.phase_mod(l)
        if self.stop_after == "mod":
            return True
        self.phase_a1(l, ctx)
        self.phase_a1(l, lat)
        if self.stop_after == "a1":
            return True
        if "YCT_in" not in self.dbg:
            if l < self.depth - 1:
                self.phase_hyena(l, ctx)
            self.phase_hyena(l, lat)
            if self.stop_after == "hyena":
                return True
        self.phase_attn(l, ctx)
        self.phase_attn(l, lat)
        if self.stop_after == "attn":
            return True
        self.phase_merge(l, ctx)
        self.phase_merge(l, lat)
        if self.stop_after == "merge":
            return True
        last = (l == self.depth - 1)
        if not last:
            self.phase_ln2(l, ctx)
            self.phase_ffn(l, ctx, ctx.XO)
        self.phase_ln2(l, lat)
        self.phase_ffn(l, lat, self.y if (last and self.stop_after is None) else lat.XO)
        if self.stop_after == "ffn":
            return True
        return False


def host_inputs(inputs, b, L):
    g = lambda k: np.asarray(inputs[k], np.float32)
    dp = g("w_mod").shape[0]
    m = {}
    m["x"] = _f32(g("x")[b])
    m["ctx"] = _f32(g("ctx")[b])
    cT = np.stack([g("c")[b].reshape(8, 128).T, g("c_ctx").reshape(8, 128).T], axis=-1)
    m["cT"] = _f32(cT)
    for k in ("w_mod", "b_mod", "norm_g", "w_in", "w_kv_up", "swa_sink", "hy_w1", "hy_w2", "hy_w3",
              "w_branch", "w_out", "w_up", "w_down"):
        m[k] = _f32(g(k))
    m["kv_norm_g"] = _f32(g("kv_norm_g").reshape(dp, 2, 128).transpose(0, 2, 1))
    m["hy_conv_w"] = _f32(g("hy_conv_w").reshape(dp, 3, 12, 128).transpose(0, 3, 2, 1))
    m["hy_conv_b"] = _f32(g("hy_conv_b").reshape(dp, 12, 128).transpose(0, 2, 1))
    m["hy_b1"] = _f32(g("hy_b1").reshape(dp, HY_HID, 1))
    m["hy_b2"] = _f32(g("hy_b2").reshape(dp, HY_HID, 1))
    m["hy_freq"] = _f32(g("hy_freq").transpose(0, 2, 1))
    m["hy_skip"] = _f32(g("hy_skip").reshape(dp, 2, 4, 128).transpose(0, 3, 1, 2))
    m["ffn_conv_w"] = _f32(g("ffn_conv_w").reshape(dp, 3, 44, 128).transpose(0, 3, 2, 1))
    m["ffn_conv_b"] = _f32(g("ffn_conv_b").reshape(dp, 44, 128).transpose(0, 2, 1))
    return m


def kernel(**inputs):
    x = np.asarray(inputs["x"])
    B, L, _ = x.shape
    P = Prog(L, NH=1, depth=DEPTH, stop_after=None)
    nc = P.build()
    maps = []
    for core in range(8):
        m = host_inputs(inputs, core % B, L)
        m.update(P.consts)
        maps.append(m)
    res = run_bass_kernel_spmd(nc, maps, core_ids=list(range(8)))
    return np.stack([np.asarray(res.results[b]["y"], np.float32) for b in range(B)], axis=0)
```
